# Optimizing a Trainium2 kernel written in Bass

```python
import math
import jax, jax.numpy as jnp
from jax import lax
import numpy as np

D_MODEL = 2048
BATCH = 4
SEQ = 2048
DEPTH = 2
DEC_BATCH = 8
DEC_SEQ = 4
PAST_LEN = 16384
PAGE_SIZE = 128

HD_A = 64
C_A = D_MODEL // 2
H_A = C_A // HD_A
LORA_W = C_A // 16
LORA_A = C_A // 16
LORA_G = C_A // 8
C_SHIFT = 3 * C_A + LORA_W + LORA_A + LORA_G
GN_EPS = 64e-5
HD_B = 128
C_B = D_MODEL // 2
H_B = C_B // HD_B
MOBA_BLOCK = 256
MOBA_TOPK = 3
Q_CHUNK = 16
N_BUCKETS = 32
MAX_DIST = 128
C_IN = C_SHIFT + 3 * C_B + 2 * D_MODEL
D_FF = 5632
CONV_W = 3
EPS = 1e-6

kernel_name = 'hybrid_rwkv7_moba_decoder_step'


def rmsnorm(x, g):
    xf = x.astype(jnp.float32)
    y = xf * lax.rsqrt(jnp.mean(xf * xf, axis=-1, keepdims=True) + EPS)
    return (y * g.astype(jnp.float32)).astype(x.dtype)


def rwkv7_mix(u, prev, S0, mu, w_decay_up, w_a_up, w_g_up, decay_base, a_base, k_k, k_a, r_k, gn_w, gn_b):
    B, T, _ = u.shape
    f32 = jnp.float32
    shifted = jnp.concatenate([prev[:, None, :].astype(u.dtype), u[:, :-1]], axis=1)
    xs = (u + (shifted - u) * mu).astype(f32)
    o1, o2, o3 = C_A, 2 * C_A, 3 * C_A
    o4 = o3 + LORA_W
    o5 = o4 + LORA_A
    r, k, v = xs[..., :o1], xs[..., o1:o2], xs[..., o2:o3]
    wd, ad, gd = xs[..., o3:o4], xs[..., o4:o5], xs[..., o5:]
    w_log = -jax.nn.softplus(-(decay_base + jnp.tanh(wd) @ w_decay_up)) - 0.5
    decay = jnp.exp(-jnp.exp(w_log))
    a = jax.nn.sigmoid(a_base + ad @ w_a_up)
    g = jax.nn.sigmoid(gd) @ w_g_up
    kk = (k * k_k).reshape(B, T, H_A, HD_A)
    kk = kk * lax.rsqrt(jnp.maximum(jnp.sum(kk * kk, axis=-1, keepdims=True), 1e-24))
    k = k * (1.0 + (a - 1.0) * k_a)
    hs = lambda z: z.reshape(B, T, H_A, HD_A)
    r, k, v, a, decay = hs(r), hs(k), hs(v), hs(a), hs(decay)

    def step(S, inp):
        r_t, w_t, k_t, v_t, kk_t, a_t = inp
        sk = jnp.einsum('bhij,bhj->bhi', S, kk_t)
        S = S * w_t[:, :, None, :] - sk[..., None] * (kk_t * a_t)[:, :, None, :] + v_t[..., None] * k_t[:, :, None, :]
        return S, jnp.einsum('bhij,bhj->bhi', S, r_t)

    tm = lambda z: jnp.moveaxis(z, 1, 0)
    S_fin, o = lax.scan(step, S0.astype(f32), (tm(r), tm(decay), tm(k), tm(v), tm(kk), tm(a)))
    o = jnp.moveaxis(o, 0, 1)
    mean = jnp.mean(o, axis=-1, keepdims=True)
    var = jnp.mean((o - mean) ** 2, axis=-1, keepdims=True)
    on = ((o - mean) * lax.rsqrt(var + GN_EPS)).reshape(B, T, C_A) * gn_w + gn_b
    bonus = (jnp.sum(r * k * r_k, axis=-1, keepdims=True) * v).reshape(B, T, C_A)
    y = (on + bonus) * g
    return y.astype(u.dtype), S_fin.astype(S0.dtype), u[:, -1]


def t5_bucket(rel):
    n = jnp.maximum(rel, 0)
    max_exact = N_BUCKETS // 2
    nf = jnp.maximum(n, 1).astype(jnp.float32)
    large = max_exact + (jnp.log(nf / max_exact) / math.log(MAX_DIST / max_exact) * (N_BUCKETS - max_exact)).astype(jnp.int32)
    large = jnp.minimum(large, N_BUCKETS - 1)
    return jnp.where(n < max_exact, n, large)


def moba_attend(q, k_all, v_all, q_pos, rel_bias):
    B, Q = q.shape[0], q.shape[1]
    L = k_all.shape[1]
    nb = -(-L // MOBA_BLOCK)
    pad = nb * MOBA_BLOCK - L

    def blocks(z):
        z = jnp.pad(z, ((0, 0), (0, pad), (0, 0), (0, 0)))
        return z.reshape(B, nb, MOBA_BLOCK, H_B, HD_B).transpose(0, 3, 1, 2, 4)

    kb, vb = blocks(k_all), blocks(v_all)
    kmean = jnp.mean(kb, axis=3, dtype=jnp.float32)
    topk = min(MOBA_TOPK, nb)
    qc = math.gcd(Q, Q_CHUNK)
    nc = Q // qc
    bi = jnp.arange(B)[:, None, None, None]
    hi = jnp.arange(H_B)[None, :, None, None]
    scale = HD_B ** -0.5
    f32 = jnp.float32

    def chunk(args):
        qx, px = args
        qh = qx.transpose(0, 2, 1, 3)
        own = px // MOBA_BLOCK
        gate = jnp.einsum('bhqd,bhnd->bhqn', qh, kmean, preferred_element_type=f32)
        past = jnp.arange(nb)[None, :] < own[:, None]
        gate = jnp.where(past, gate, -jnp.inf)
        top_s, top_i = lax.top_k(gate, topk)
        own_b = jnp.broadcast_to(own[None, None, :, None], (B, H_B, qc, 1))
        idx = jnp.concatenate([top_i, own_b], axis=-1)
        valid = jnp.concatenate([jnp.isfinite(top_s), jnp.ones((B, H_B, qc, 1), bool)], axis=-1)
        kg = kb[bi, hi, idx]
        vg = vb[bi, hi, idx]
        kpos = idx[..., None] * MOBA_BLOCK + jnp.arange(MOBA_BLOCK)
        rel = px[:, None, None] - kpos
        mask = valid[..., None] & (rel >= 0)
        bias = rel_bias[hi[..., None], t5_bucket(rel)].astype(f32)
        s = jnp.einsum('bhqd,bhqnkd->bhqnk', qh, kg, preferred_element_type=f32) * scale + bias
        s = jnp.where(mask, s, -jnp.inf)
        p = jax.nn.softmax(s.reshape(B, H_B, qc, -1), axis=-1).reshape(s.shape)
        o = jnp.einsum('bhqnk,bhqnkd->bhqd', p.astype(vg.dtype), vg, preferred_element_type=f32)
        return o.transpose(0, 2, 1, 3).astype(q.dtype)

    qs = q.reshape(B, nc, qc, H_B, HD_B).transpose(1, 0, 2, 3, 4)
    ps = q_pos.reshape(nc, qc)
    o = lax.map(chunk, (qs, ps))
    return o.transpose(1, 0, 2, 3, 4).reshape(B, Q, C_B)


def conv_ffn(h, buf, w_up, conv_w, conv_b, w_down):
    T = h.shape[1]
    ug = h @ w_up
    u, gt = ug[..., :D_FF], ug[..., D_FF:]
    ext = jnp.concatenate([buf.astype(u.dtype), u], axis=1)
    c = conv_b + sum(conv_w[j] * ext[:, j:j + T] for j in range(CONV_W))
    y = (jax.nn.gelu(c, approximate=True) * gt) @ w_down
    return y, ext[:, T:]


def layer_forward(x, q_pos, S0, shift0, conv0, past_k, past_v, rel_bias, lp):
    B, T, _ = x.shape
    h = rmsnorm(x, lp['ln_attn_pre'])
    proj = h @ lp['w_in']
    o_b = C_SHIFT + 3 * C_B
    y_a, S_new, shift_new = rwkv7_mix(proj[..., :C_SHIFT], shift0, S0, lp['mu_shift'], lp['w_decay_up'], lp['w_a_up'],
                                      lp['w_g_up'], lp['decay_base'], lp['a_base'], lp['k_k'], lp['k_a'], lp['r_k'],
                                      lp['gn_w'], lp['gn_b'])
    qkv = proj[..., C_SHIFT:o_b].reshape(B, T, 3, H_B, HD_B)
    q, k, v = qkv[:, :, 0], qkv[:, :, 1], qkv[:, :, 2]
    k_all = k if past_k is None else jnp.concatenate([past_k.astype(k.dtype), k], axis=1)
    v_all = v if past_v is None else jnp.concatenate([past_v.astype(v.dtype), v], axis=1)
    y_b = moba_attend(q, k_all, v_all, q_pos, rel_bias)
    g_a = jax.nn.sigmoid(proj[..., o_b:o_b + D_MODEL])
    g_b = jax.nn.sigmoid(proj[..., o_b + D_MODEL:])
    mix = g_a * (y_a @ lp['w_branch_a']) + g_b * (y_b @ lp['w_branch_b'])
    x = x + rmsnorm(mix @ lp['w_out'], lp['ln_attn_post'])
    f, conv_new = conv_ffn(rmsnorm(x, lp['ln_ffn_pre']), conv0, lp['w_ffn_up'], lp['ffn_conv_w'], lp['ffn_conv_b'], lp['w_ffn_down'])
    x = x + rmsnorm(f, lp['ln_ffn_post'])
    return x, k, v, S_new, shift_new, conv_new


def setup_inputs(seed: int = 0) -> dict:
    key = jax.random.key(seed)
    ks = jax.random.split(key, 40)
    nrm = lambda i, shape, scale: jax.random.normal(ks[i], shape, jnp.float32) * scale
    n_pages = PAST_LEN // PAGE_SIZE
    used = DEC_BATCH * n_pages
    n_pool = used + max(1, used // 4)
    page_table = jax.random.permutation(ks[0], n_pool)[:used].reshape(DEC_BATCH, n_pages).astype(jnp.int32)
    return {
        'x_prompt': nrm(1, (BATCH, SEQ, D_MODEL), 1.0),
        'x_sample': nrm(2, (DEC_BATCH, DEC_SEQ, D_MODEL), 1.0),
        'state_wkv': nrm(3, (DEPTH, DEC_BATCH, H_A, HD_A, HD_A), 0.5),
        'state_shift': nrm(4, (DEPTH, DEC_BATCH, C_SHIFT), 1.0),
        'state_conv': nrm(5, (DEPTH, DEC_BATCH, CONV_W - 1, D_FF), 1.0),
        'cache_k': nrm(6, (DEPTH, n_pool, PAGE_SIZE, H_B, HD_B), 1.0),
        'cache_v': nrm(7, (DEPTH, n_pool, PAGE_SIZE, H_B, HD_B), 1.0),
        'page_table': page_table,
        'rel_bias': nrm(8, (H_B, N_BUCKETS), 0.5),
        'ln_attn_pre': 1.0 + nrm(9, (DEPTH, D_MODEL), 0.05),
        'ln_attn_post': 1.0 + nrm(10, (DEPTH, D_MODEL), 0.05),
        'ln_ffn_pre': 1.0 + nrm(11, (DEPTH, D_MODEL), 0.05),
        'ln_ffn_post': 1.0 + nrm(12, (DEPTH, D_MODEL), 0.05),
        'w_in': nrm(13, (DEPTH, D_MODEL, C_IN), D_MODEL ** -0.5),
        'mu_shift': jax.random.uniform(ks[14], (DEPTH, C_SHIFT), jnp.float32),
        'w_decay_up': nrm(15, (DEPTH, LORA_W, C_A), LORA_W ** -0.5),
        'w_a_up': nrm(16, (DEPTH, LORA_A, C_A), LORA_A ** -0.5),
        'w_g_up': nrm(17, (DEPTH, LORA_G, C_A), LORA_G ** -0.5),
        'decay_base': jax.random.uniform(ks[18], (DEPTH, C_A), jnp.float32, minval=-3.0, maxval=1.0),
        'a_base': nrm(19, (DEPTH, C_A), 0.1),
        'k_k': 0.85 + nrm(20, (DEPTH, C_A), 0.05),
        'k_a': 1.0 + nrm(21, (DEPTH, C_A), 0.05),
        'r_k': nrm(22, (DEPTH, H_A, HD_A), 0.1),
        'gn_w': 1.0 + nrm(23, (DEPTH, C_A), 0.05),
        'gn_b': nrm(24, (DEPTH, C_A), 0.02),
        'w_branch_a': nrm(25, (DEPTH, C_A, D_MODEL), C_A ** -0.5),
        'w_branch_b': nrm(26, (DEPTH, C_B, D_MODEL), C_B ** -0.5),
        'w_out': nrm(27, (DEPTH, D_MODEL, D_MODEL), D_MODEL ** -0.5),
        'w_ffn_up': nrm(28, (DEPTH, D_MODEL, 2 * D_FF), D_MODEL ** -0.5),
        'ffn_conv_w': nrm(29, (DEPTH, CONV_W, D_FF), CONV_W ** -0.5),
        'ffn_conv_b': nrm(30, (DEPTH, D_FF), 0.02),
        'w_ffn_down': nrm(31, (DEPTH, D_FF, D_MODEL), D_FF ** -0.5),
    }


def reference(x_prompt, x_sample, state_wkv, state_shift, state_conv, cache_k, cache_v, page_table, rel_bias,
              ln_attn_pre, ln_attn_post, ln_ffn_pre, ln_ffn_post, w_in, mu_shift, w_decay_up, w_a_up, w_g_up,
              decay_base, a_base, k_k, k_a, r_k, gn_w, gn_b, w_branch_a, w_branch_b, w_out, w_ffn_up, ffn_conv_w,
              ffn_conv_b, w_ffn_down):
    bp, tp = x_prompt.shape[0], x_prompt.shape[1]
    bs, ts = x_sample.shape[0], x_sample.shape[1]
    past_len = page_table.shape[1] * cache_k.shape[2]
    pos_p = jnp.arange(tp, dtype=jnp.int32)
    pos_s = past_len + jnp.arange(ts, dtype=jnp.int32)
    dt = x_prompt.dtype
    xp, xs = x_prompt, x_sample
    kp, vp, wp, shp, cvp = [], [], [], [], []
    ksm, vsm, wsm, shs, cvs = [], [], [], [], []
    for l in range(DEPTH):
        lp = dict(ln_attn_pre=ln_attn_pre[l], ln_attn_post=ln_attn_post[l], ln_ffn_pre=ln_ffn_pre[l],
                  ln_ffn_post=ln_ffn_post[l], w_in=w_in[l], mu_shift=mu_shift[l], w_decay_up=w_decay_up[l],
                  w_a_up=w_a_up[l], w_g_up=w_g_up[l], decay_base=decay_base[l], a_base=a_base[l], k_k=k_k[l],
                  k_a=k_a[l], r_k=r_k[l], gn_w=gn_w[l], gn_b=gn_b[l], w_branch_a=w_branch_a[l],
                  w_branch_b=w_branch_b[l], w_out=w_out[l], w_ffn_up=w_ffn_up[l], ffn_conv_w=ffn_conv_w[l],
                  ffn_conv_b=ffn_conv_b[l], w_ffn_down=w_ffn_down[l])
        xp, k1, v1, s1, sh1, c1 = layer_forward(
            xp, pos_p, jnp.zeros((bp, H_A, HD_A, HD_A), dt), jnp.zeros((bp, C_SHIFT), dt),
            jnp.zeros((bp, CONV_W - 1, D_FF), dt), None, None, rel_bias, lp)
        pk = cache_k[l][page_table].reshape(bs, past_len, H_B, HD_B)
        pv = cache_v[l][page_table].reshape(bs, past_len, H_B, HD_B)
        xs, k2, v2, s2, sh2, c2 = layer_forward(
            xs, pos_s, state_wkv[l], state_shift[l], state_conv[l], pk, pv, rel_bias, lp)
        kp.append(k1); vp.append(v1); wp.append(s1); shp.append(sh1); cvp.append(c1)
        ksm.append(k2); vsm.append(v2); wsm.append(s2); shs.append(sh2); cvs.append(c2)
    return (xp, xs, jnp.stack(kp), jnp.stack(vp), jnp.stack(wp), jnp.stack(shp), jnp.stack(cvp),
            jnp.stack(ksm), jnp.stack(vsm), jnp.stack(wsm), jnp.stack(shs), jnp.stack(cvs))
```

```python
import contextlib
import numpy as np
import concourse.bass as bass
import concourse.mybir as mybir
from concourse.bass_utils import run_bass_kernel_spmd

F32 = mybir.dt.float32
BF16 = mybir.dt.bfloat16
I32 = mybir.dt.int32
AF = mybir.ActivationFunctionType
ALU = mybir.AluOpType
AX = mybir.AxisListType


class Buf:
    __slots__ = ("t", "name", "w", "r", "live", "multi", "h")

    def __init__(self, t, name, init_r=None):
        self.t = t
        self.name = name
        self.w = {}
        self.r = dict(init_r or {})
        self.live = False
        self.multi = False
        self.h = None

    def __getitem__(self, idx):
        return self.t[idx]


class Prog:
    def __init__(self, nc):
        self.nc = nc
        self.es = contextlib.ExitStack()
        self.eng = {"pe": nc.tensor, "act": nc.scalar, "dve": nc.vector, "pool": nc.gpsimd, "sp": nc.sync}
        self.sems = {}
        self.cnt = {}
        self.waited = {e: {} for e in self.eng}
        self.freed = {}
        self.nops = {e: 0 for e in self.eng}
        for e in ("pe", "act", "dve", "pool"):
            self._sem(e)
        self.psum_banks = []
        self.psum_i = 0

    def _sem(self, key):
        if key not in self.sems:
            self.sems[key] = self.es.enter_context(self.nc.semaphore("s_" + key))
            self.cnt[key] = 0
        return self.sems[key]

    def sbuf(self, name, shape, dtype, stack=None):
        self.uid = getattr(self, "uid", 0) + 1
        nb = int(np.prod(shape[1:])) * (2 if dtype == BF16 else 4)
        self.acct = getattr(self, "acct", {})
        key = "persist" if stack is None else "scope"
        self.acct[key] = self.acct.get(key, 0) + nb
        self.big = getattr(self, "big", [])
        self.big.append((nb, name, key))
        t = (stack or self.es).enter_context(self.nc.sbuf_tensor(f"sb{self.uid}_{name}", list(shape), dtype))
        return Buf(t, name, self.freed)

    def dram(self, name, shape, dtype, kind):
        t = self.nc.dram_tensor(name, list(shape), dtype, kind=kind)
        b = Buf(t.ap(), name)
        b.h = t
        return b

    def init_psum(self):
        for i in range(8):
            t = self.es.enter_context(self.nc.psum_tensor(f"ps{i}", [128, 512], F32))
            self.psum_banks.append(Buf(t, f"ps{i}"))

    def psum(self):
        while True:
            b = self.psum_banks[self.psum_i % 8]
            self.psum_i += 1
            if not b.live:
                return b

    def psum_hold(self):
        b = self.psum()
        b.live = True
        return b

    @contextlib.contextmanager
    def scope(self):
        st = contextlib.ExitStack()
        bufs = []

        def alloc(name, shape, dtype):
            b = self.sbuf(name, shape, dtype, stack=st)
            bufs.append(b)
            return b
        try:
            yield alloc
        finally:
            fr = dict(self.freed)
            for b in bufs:
                for d in (b.w, b.r):
                    for k, v in d.items():
                        if fr.get(k, 0) < v:
                            fr[k] = v
            self.freed = fr
            st.close()

    def _deps(self, eng, reads, writes):
        deps = {}

        def add(d):
            for k, v in d.items():
                if deps.get(k, 0) < v:
                    deps[k] = v
        for b in reads:
            add(b.w)
        for b in writes:
            if b.multi:
                continue
            add(b.w)
            add(b.r)
        out = []
        wd = self.waited[eng]
        for k, v in deps.items():
            if k == "pe" and eng == "pe":
                continue
            if wd.get(k, 0) >= v:
                continue
            wd[k] = v
            out.append((k, v))
        return out

    def _commit(self, tok, reads, writes):
        k, v = tok
        for b in writes:
            if b.multi:
                if b.w.get(k, 0) < v:
                    b.w[k] = v
                continue
            b.w = {k: v}
            b.r = {}
        for b in reads:
            if b.r.get(k, 0) < v:
                b.r[k] = v

    def op(self, eng, fn, reads=(), writes=()):
        E = self.eng[eng]
        for k, v in self._deps(eng, reads, writes):
            E.wait_ge(self.sems[k], v)
        inst = fn(E)
        self.cnt[eng] += 1
        inst.then_inc(self.sems[eng], 1)
        self.nops[eng] += 1
        self._commit((eng, self.cnt[eng]), reads, writes)

    def dma(self, queue, stream, pairs, reads=(), writes=(), **kw):
        E = self.eng[queue]
        self._sem(stream)
        for k, v in self._deps(queue, reads, writes):
            E.wait_ge(self.sems[k], v)
        for (o, i) in pairs:
            E.dma_start(out=o, in_=i, **kw).then_inc(self.sems[stream], 16)
            self.cnt[stream] += 16
            self.nops[queue] += 1
        self._commit((stream, self.cnt[stream]), reads, writes)

    def dma_gather(self, stream, out_ap, in_ap, idx_ap, reads=(), writes=()):
        E = self.eng["pool"]
        self._sem(stream)
        for k, v in self._deps("pool", reads, writes):
            E.wait_ge(self.sems[k], v)
        E.indirect_dma_start(out=out_ap, out_offset=None, in_=in_ap,
                             in_offset=bass.IndirectOffsetOnAxis(ap=idx_ap, axis=0)).then_inc(self.sems[stream], 16)
        self.cnt[stream] += 16
        self.nops["pool"] += 1
        self._commit((stream, self.cnt[stream]), reads, writes)

    def finish(self):
        E = self.eng["sp"]
        for k, v in self.cnt.items():
            if v > 0 and self.waited["sp"].get(k, 0) < v:
                E.wait_ge(self.sems[k], v)

    def mm(self, ps, out, lhsT, rhs, start, stop, reads):
        self.op("pe", lambda e: e.matmul(out, lhsT, rhs, start=start, stop=stop), reads=reads, writes=[ps])

    def act(self, out, in_, func, reads, writes, eng="act", **kw):
        self.op(eng, lambda e: e.activation(out, in_, func, **kw), reads=reads, writes=writes)

    def ts(self, eng, out, in0, s1, s2, op0, op1, reads, writes):
        if op1 is None:
            self.op(eng, lambda e: e.tensor_scalar(out, in0, s1, None, op0=op0), reads=reads, writes=writes)
        else:
            self.op(eng, lambda e: e.tensor_scalar(out, in0, s1, s2, op0=op0, op1=op1), reads=reads, writes=writes)

    def tt(self, eng, out, in0, in1, op, reads, writes):
        self.op(eng, lambda e: e.tensor_tensor(out, in0, in1, op=op), reads=reads, writes=writes)

    def stt(self, eng, out, in0, scalar, in1, op0, op1, reads, writes):
        self.op(eng, lambda e: e.scalar_tensor_tensor(out, in0, scalar, in1, op0=op0, op1=op1), reads=reads, writes=writes)

    def copy(self, eng, out, in_, reads, writes):
        if eng == "act":
            self.op(eng, lambda e: e.activation(out, in_, AF.Copy), reads=reads, writes=writes)
        else:
            self.op(eng, lambda e: e.tensor_copy(out, in_), reads=reads, writes=writes)
D = 2048
C_A = 1024
C_SHIFT = 3328
D_FF = 5632
NFF = 44
SCALE = 128 ** -0.5
NEG = -30000.0
DEC_K = float(np.exp(-0.5))
GELU_K = float(2.0 * np.sqrt(2.0 / np.pi))

V_LN1, V_LN2, V_LN3, V_LN4 = 0, 16, 32, 48
V_MU = 64
V_DB = 90
V_AB = 98
V_KK = 106
V_KA = 114
V_RK = 122
V_GW = 130
V_GB = 138
V_CW = 146
V_CB = 278
NV = 322


class Cfg:
    def __init__(self, T=2048, NT=512, L=2, sample=True, n_pool=1280, n_pages=128, en_rwkv=True, en_moba=True):
        self.T, self.NT, self.L, self.sample = T, NT, L, sample
        self.n_pool, self.n_pages = n_pool, n_pages
        self.en_rwkv, self.en_moba = en_rwkv, en_moba
EPS = 1e-6
GN_EPS = 64e-5
NW16 = 3


class Group:
    pass


def build(cfg):
    T, NT, L = cfg.T, cfg.NT, cfg.L
    nc = bass.Bass("TRN2", target_bir_lowering=False)
    P = Prog(nc)
    global LASTP
    LASTP = P
    P.init_psum()
    dI = lambda n, s, dt=F32: P.dram(n, s, dt, "ExternalInput")
    dO = lambda n, s, dt=F32: P.dram(n, s, dt, "ExternalOutput")
    xTp = dI("xTp", [D, T])
    win = [dI(f"win{l}", [82, 128, 16, 128]) for l in range(L)]
    wvr = [dI(f"wvr{l}", [4, 128, 16, 256]) for l in range(L)]
    wa = [dI(f"wa{l}", [16, 128, 8, 128]) for l in range(L)]
    wb = [dI(f"wb{l}", [16, 128, 8, 128]) for l in range(L)]
    wo = [dI(f"wo{l}", [16, 128, 16, 128]) for l in range(L)]
    wup = [dI(f"wup{l}", [88, 128, 16, 128]) for l in range(L)]
    wdn = [dI(f"wdn{l}", [16, 128, 44, 128]) for l in range(L)]
    vecs_d = dI("vecs", [L, 128, NV])
    lora_d = dI("lora", [L, 128, 2048])
    cst_d = dI("cst", [128, NCST])
    relb_d = dI("relb", [64, 8])
    onehot_d = dI("onehot", [64, 512])
    yTp = dO("yTp", [D, T])
    kTo = dO("kTo", [L, 1024, T])
    vo = dO("vo", [L, T, 1024])
    wkv_o = dO("wkv_o", [L, 128, 8, 64])
    shift_o = dO("shift_o", [L, 128, 26])
    conv_o = dO("conv_o", [L, 128, NFF, 2])
    tz_d = [P.dram(f"tz_scr{h}", [128, 384], F32, "Internal") for h in range(8)]
    for b_ in (kTo, vo, wkv_o, shift_o, conv_o, yTp):
        b_.multi = True
    extra = {}
    if getattr(cfg, "debug", False):
        extra["dbg_o"] = dO("dbg_o", [128, 8, NT])
        extra["dbg_ya"] = dO("dbg_ya", [8, 128, NT], BF16)
        extra["dbg_kr"] = dO("dbg_kr", [8, 128, NT * 2], BF16)
        extra["dbg_x"] = dO("dbg_x", [3, 128, NT])
    if cfg.sample:
        extra["xTs"] = dI("xTs", [D, 4])
        extra["ptr"] = dI("ptr", [128, cfg.n_pages], I32)
        extra["ohs"] = dI("ohs", [64, 4, 132])
        for l_ in range(L):
            extra[f"cache_k{l_}"] = dI(f"cache_k{l_}", [cfg.n_pool * 128, 1024])
            extra[f"cache_v{l_}"] = dI(f"cache_v{l_}", [cfg.n_pool * 128, 1024])
        extra["ks_o"] = dO("ks_o", [L, 128, 8, 4])
        extra["vs_o"] = dO("vs_o", [L, 4, 1024])
        extra["ks_o"].multi = True
        extra["vs_o"].multi = True
        extra["s_shift"] = dI("s_shift", [L, 128, 26])
        extra["s_wkv"] = dI("s_wkv", [L, 128, 8, 64])
        extra["s_conv"] = dI("s_conv", [L, 128, NFF, 2])
        extra["yTs"] = dO("yTs", [D, 4])
        extra["wkv_os"] = dO("wkv_os", [L, 128, 8, 64])
        extra["shift_os"] = dO("shift_os", [L, 128, 26])
        extra["conv_os"] = dO("conv_os", [L, 128, NFF, 2])
        for k_ in ("wkv_os", "shift_os", "conv_os"):
            extra[k_].multi = True

    with nc.Block() as block:
        @block.sync
        def _(_e):
            emit(cfg, nc, P, locals_ := dict(xTp=xTp, win=win, wvr=wvr, wa=wa, wb=wb, wo=wo, wup=wup, wdn=wdn, vecs_d=vecs_d,
                                          lora_d=lora_d, cst_d=cst_d, relb_d=relb_d, onehot_d=onehot_d, yTp=yTp, kTo=kTo, vo=vo,
                                          wkv_o=wkv_o, shift_o=shift_o, conv_o=conv_o, tz=tz_d, **extra))
            P.finish()
    P.es.close()
    return nc, P


CST_ID = 0
CST_MAB = 128
CST_ML = 640
CST_I8 = 1152
CST_RM = 1664
CST_OHQ = 2304
CST_PIDX = 2816
CST_MAB4 = 2176
CST_ML4 = 2240
CST_I84 = 2272
NCST = 2820


def make_cst():
    c = np.zeros((128, NCST), np.float32)
    c[:, CST_ID:CST_ID + 128] = np.eye(128, dtype=np.float32)
    s = np.arange(64)[:, None]
    t = np.arange(64)[None, :]
    mab = np.concatenate([(s < t), (s <= t)], axis=1).astype(np.float32)
    c[:64, CST_MAB:CST_MAB + 512] = np.tile(mab, (1, 4))
    ml = (t < s).astype(np.float32)
    c[:64, CST_ML:CST_ML + 512] = np.tile(ml, (1, 8))
    c[:64, CST_I8:CST_I8 + 512] = np.tile(np.eye(64, dtype=np.float32), (1, 8))
    rm = np.ones(512, np.float32)
    rm[::64] = 0.0
    c[:, CST_RM:CST_RM + 512] = rm[None, :]
    s4 = np.arange(4)[:, None]
    t4 = np.arange(4)[None, :]
    c[:4, CST_MAB4:CST_MAB4 + 64] = np.tile(np.concatenate([(s4 < t4), (s4 <= t4)], axis=1).astype(np.float32), (1, 8))
    c[:4, CST_ML4:CST_ML4 + 32] = np.tile((t4 < s4).astype(np.float32), (1, 8))
    c[:4, CST_I84:CST_I84 + 32] = np.tile(np.eye(4, dtype=np.float32), (1, 8))
    for qi in range(4):
        c[qi, CST_OHQ + qi * 128:CST_OHQ + (qi + 1) * 128] = 1.0
    c[:, CST_PIDX] = np.arange(128, dtype=np.float32)
    return c


def emit(cfg, nc, P, d):
    T, NT, L = cfg.T, cfg.NT, cfg.L
    NS = NT // 128
    ntiles = T // NT
    xTp, win, wvr, wa, wb, wo, wup, wdn = d["xTp"], d["win"], d["wvr"], d["wa"], d["wb"], d["wo"], d["wup"], d["wdn"]
    yTp, kTo, vo, wkv_o, shift_o, conv_o = d["yTp"], d["kTo"], d["vo"], d["wkv_o"], d["shift_o"], d["conv_o"]

    cst = P.sbuf("cst", [128, NCST], F32)
    vec = [P.sbuf(f"vec{l}", [128, NV], F32) for l in range(L)]
    lora1 = P.sbuf("lora", [128, 2048], F32)
    lora = [lora1 for l in range(L)]
    P.dma("sp", "ld_c", [(cst[:, :], d["cst_d"][:, :])] + [(vec[l][:, :], d["vecs_d"][l]) for l in range(L)]
, reads=[d["cst_d"]], writes=[cst] + vec)
    cstb = P.sbuf("cstb", [128, 128], BF16)
    P.copy("dve", cstb[:, :], cst[:, CST_ID:CST_ID + 128], reads=[cst], writes=[cstb])
    ones_f = P.sbuf("ones_f", [128, 128], F32)
    P.op("dve", lambda e: e.memset(ones_f[:, :], 1.0), writes=[ones_f])
    bones = P.sbuf("bones", [128, 128], F32)
    P.op("dve", lambda e: e.memset(bones[:, :], 0.0), writes=[bones])
    P.op("dve", lambda e: e.memset(bones[0:64, 0:64], 1.0), writes=[bones])
    P.op("dve", lambda e: e.memset(bones[64:128, 64:128], 1.0), writes=[bones])
    identf = lambda: cst[:, CST_ID:CST_ID + 128]
    identb = lambda: cstb[:, :]

    w16 = [P.sbuf(f"w16_{i}", [128, 16, 128], BF16) for i in range(NW16)]
    w44 = []
    wvb = []
    wstate = {"i16": 0, "i44": 0, "iv": 0, "tile": 0}

    wcache = {}
    wq = {"i": 0}

    def getw(src, idx, kc):
        if src.name not in wcache:
            shp = [int(x) for x in src.t.shape]
            c = P.dram("wc_" + src.name, shp, BF16, "Internal")
            c.multi = True
            wcache[src.name] = c
        cache = wcache[src.name]
        first = wstate["tile"] == 0
        if kc == 44:
            k = wstate["i44"] % 2
            b = w44[k]
            wstate["i44"] += 1
            sname, dst = f"w44s{k}", b[:, :, :]
        elif kc == "v":
            k = 0
            b = wvb[0]
            sname, dst = "wvs", b[:, :, :]
        else:
            k = wstate["i16"] % NW16
            b = w16[k]
            wstate["i16"] += 1
            sname, dst = f"w16s{k}", b[:, 0:kc, :]
        if first:
            P.dma("pool", sname, [(dst, src[idx])], reads=[src], writes=[b])
            P.dma("sp", sname + "c", [(cache.t[idx], dst)], reads=[b], writes=[cache])
        else:
            wq["i"] += 1
            if wq["i"] % 2:
                P.dma("pool", sname, [(dst, cache.t[idx])], reads=[cache], writes=[b])
            else:
                P.dma("sp", sname + "h", [(dst, cache.t[idx])], reads=[cache], writes=[b])
        return b

    def projm(src, idx, kc, groups, ins):
        w = getw(src, idx, kc)
        res = []
        for g, hin in zip(groups, ins):
            ps = P.psum()
            for k in range(kc):
                P.mm(ps, ps[:, 0:g.N], w[:, k, :], hin[k][:, 0:g.N], k == 0, k == kc - 1, reads=[w, hin[k]])
            res.append(ps)
        return res

    def mkgroup(name, N, C):
        g = Group()
        g.name, g.N, g.C = name, N, C
        g.NCH = N // C
        g.nlev = int(np.log2(C))
        g.x = [P.sbuf(f"{name}_x{k}", [128, N], F32) for k in range(16)]
        g.h = [P.sbuf(f"{name}_h{k}", [128, N], BF16) for k in range(16)]
        g.tmpi = 0
        g.tmps = [P.sbuf(f"{name}_t{k}", [128, N], F32) for k in range(6)]
        g.shift = [P.sbuf(f"{name}_sh{l}", [128, 26], F32) for l in range(L)]
        g.Zm = [P.sbuf(f"{name}_Zm{l}", [128, 8, 64], F32) for l in range(L)]
        g.Zs_e = [[P.sbuf(f"{name}_Zs{l}e{e}", [128, 8, 64], BF16) for e in (0, 1)] for l in range(L)]
        g.convp = [P.sbuf(f"{name}_cv{l}", [128, NFF, 2], F32) for l in range(L)]
        g.ya = [P.sbuf(f"{name}_ya{k}", [128, N], BF16) for k in range(8)]
        g.yb = [P.sbuf(f"{name}_yb{k}", [128, N], BF16) for k in range(8)]
        return g

    def tmp(g):
        b = g.tmps[g.tmpi % len(g.tmps)]
        g.tmpi += 1
        return b

    def sumsq_rstd(g, X, rstd):
        N = g.N
        ps = P.psum()
        for k in range(16):
            sq = tmp(g)
            P.act(sq[:, 0:N], X[k][:, 0:N], AF.Square, reads=[X[k]], writes=[sq])
            P.mm(ps, ps[:, 0:N], ones_f[:, :], sq[:, 0:N], k == 0, k == 15, reads=[ones_f, sq])
        P.ts("dve", rstd[:, 0:N], ps[:, 0:N], 1.0 / D, EPS, ALU.mult, ALU.add, reads=[ps], writes=[rstd])
        P.op("dve", lambda e: e.reciprocal(rstd[:, 0:N], rstd[:, 0:N]), reads=[rstd], writes=[rstd])
        P.act(rstd[:, 0:N], rstd[:, 0:N], AF.Sqrt, reads=[rstd], writes=[rstd])

    def norm_pre(g, l, vcol):
        N = g.N
        rstd = g.rstd
        sumsq_rstd(g, g.x, rstd)
        for k in range(16):
            P.stt("dve", g.h[k][:, 0:N], g.x[k][:, 0:N], vec[l][:, vcol + k:vcol + k + 1], rstd[:, 0:N], ALU.mult, ALU.mult,
                  reads=[g.x[k], vec[l], rstd], writes=[g.h[k]])

    def norm_post_add(g, l, vcol, M):
        N = g.N
        rstd = g.rstd
        sumsq_rstd(g, M, rstd)
        for k in range(16):
            t = tmp(g)
            P.stt("dve", t[:, 0:N], M[k][:, 0:N], vec[l][:, vcol + k:vcol + k + 1], rstd[:, 0:N], ALU.mult, ALU.mult,
                  reads=[M[k], vec[l], rstd], writes=[t])
            P.tt("dve", g.x[k][:, 0:N], g.x[k][:, 0:N], t[:, 0:N], ALU.add, reads=[g.x[k], t], writes=[g.x[k]])

    def shiftmix(g, l, ch, ps, out, uext):
        N = g.N
        P.copy("act", uext[:, 1:N + 1], ps[:, 0:N], reads=[ps], writes=[uext])
        P.copy("dve", uext[:, 0:1], g.shift[l][:, ch:ch + 1], reads=[g.shift[l], uext], writes=[uext])
        P.copy("dve", g.shift[l][:, ch:ch + 1], uext[:, N:N + 1], reads=[uext, g.shift[l]], writes=[g.shift[l]])
        dd = tmp(g)
        P.tt("dve", dd[:, 0:N], uext[:, 0:N], uext[:, 1:N + 1], ALU.subtract, reads=[uext], writes=[dd])
        P.stt("dve", out[:, 0:N], dd[:, 0:N], vec[l][:, V_MU + ch:V_MU + ch + 1], uext[:, 1:N + 1], ALU.mult, ALU.add,
              reads=[dd, vec[l], uext], writes=[out])

    def layer(l, groups, tile_i, last):
        t0 = tile_i * NT
        for g in groups:
            norm_pre(g, l, V_LN1)
        hin = [g.h for g in groups]
        with P.scope() as alloc:
            for g in groups:
                N = g.N
                g.KR = [alloc(f"{g.name}_KR{i}", [128, g.NCH, 2 * g.C], BF16) for i in range(8)]
                g.KT = [alloc(f"{g.name}_KT{i}", [128, N], BF16) for i in range(8)]
                g.BT = [alloc(f"{g.name}_BT{i}", [128, N], BF16) for i in range(8)]
                g.VF = [alloc(f"{g.name}_VF{i}", [128, N], BF16) for i in range(8)]
                g.G = [alloc(f"{g.name}_G{i}", [128, N], F32) for i in range(8)]
                g.bonus = [alloc(f"{g.name}_bo{i}", [128, N], F32) for i in range(8)]
                g.gC = alloc(f"{g.name}_gC", [128, 8, g.NCH], F32)
                g.oT = alloc(f"{g.name}_oT", [128, 8, N], F32)
                g.wk = [alloc(f"{g.name}_wk{i}", [128, N], F32) for i in range(2)]
            if cfg.en_rwkv:
                P.dma("sp", "ld_lora", [(lora1[:, :], d["lora_d"][l])], reads=[d["lora_d"]], writes=[lora1])
                with P.scope() as alloc2:
                    for g in groups:
                        N = g.N
                        g.uext = [alloc2(f"{g.name}_ue{i}", [128, N + 1], F32) for i in range(2)]
                        g.LA = alloc2(f"{g.name}_LA", [128, N], F32)
                        g.LG = alloc2(f"{g.name}_LG", [128, N], F32)
                        g.xr = alloc2(f"{g.name}_xr", [128, N], F32)
                        g.xk = alloc2(f"{g.name}_xk", [128, N], F32)
                        g.xv = alloc2(f"{g.name}_xv", [128, N], F32)
                        g.wk = [alloc2(f"{g.name}_wk{i}", [128, N], F32) for i in range(10)]
                    rwkv_pre(l, groups, hin)
                for g in groups:
                    with P.scope() as alloc2:
                        C = g.C
                        S = Group()
                        nm = f"{g.name}_cb"
                        S.Vtok = alloc2(nm + "V", [64, 1024], BF16)
                        S.Ktok = alloc2(nm + "K", [64, 1024], BF16)
                        S.Btok = alloc2(nm + "B", [64, 1024], BF16)
                        S.AkBk = [alloc2(nm + f"AkBk{e}", [64, 8, 2 * C], BF16) for e in (0, 1)]
                        S.AbBb = [alloc2(nm + f"AbBb{e}", [64, 8, 2 * C], BF16) for e in (0, 1)]
                        S.Pm = [[alloc2(nm + f"P{e}{k}", [64, 8, C], BF16) for k in (0, 1)] for e in (0, 1)]
                        S.Qm = [[alloc2(nm + f"Q{e}{k}", [64, 8, C], BF16) for k in (0, 1)] for e in (0, 1)]
                        S.Nm = [alloc2(nm + f"Nm{e}", [64, 8, C], F32) for e in (0, 1)]
                        S.Ns = [alloc2(nm + f"Ns{e}", [64, 8, C], BF16) for e in (0, 1)]
                        S.RHS = [alloc2(nm + f"RHS{e}", [64, 8, 64], BF16) for e in (0, 1)]
                        S.Un = [alloc2(nm + f"Un{e}", [64, 8, 64], BF16) for e in (0, 1)]
                        g.cb = [S, S]
                        rwkv_scan(l, g)
                    rwkv_out(l, g)
            else:
                for g in groups:
                    for k in range(8):
                        P.op("dve", lambda e: e.memset(g.ya[k][:, :], 0.0), writes=[g.ya[k]])
        with P.scope() as alloc:
            if cfg.en_moba:
                moba(l, groups, hin, tile_i, alloc)
            else:
                for g in groups:
                    for k in range(8):
                        P.op("dve", lambda e: e.memset(g.yb[k][:, :], 0.0), writes=[g.yb[k]])
        with P.scope() as alloc:
            for g in groups:
                g.mix = [alloc(f"{g.name}_mix{k}", [128, g.N], BF16) for k in range(16)]
                g.m2 = [alloc(f"{g.name}_m2{k}", [128, g.N], F32) for k in range(16)]
            for n in range(16):
                psA = projm(wa[l], n, 8, groups, [g.ya for g in groups])
                psB = projm(wb[l], n, 8, groups, [g.yb for g in groups])
                psGa = projm(win[l], 50 + n, 16, groups, hin)
                psGb = projm(win[l], 66 + n, 16, groups, hin)
                for gi, g in enumerate(groups):
                    N = g.N
                    ga, gb = tmp(g), tmp(g)
                    P.act(ga[:, 0:N], psGa[gi][:, 0:N], AF.Sigmoid, reads=[psGa[gi]], writes=[ga])
                    P.act(gb[:, 0:N], psGb[gi][:, 0:N], AF.Sigmoid, reads=[psGb[gi]], writes=[gb])
                    P.tt("dve", ga[:, 0:N], ga[:, 0:N], psA[gi][:, 0:N], ALU.mult, reads=[ga, psA[gi]], writes=[ga])
                    P.tt("dve", gb[:, 0:N], gb[:, 0:N], psB[gi][:, 0:N], ALU.mult, reads=[gb, psB[gi]], writes=[gb])
                    P.tt("dve", g.mix[n][:, 0:N], ga[:, 0:N], gb[:, 0:N], ALU.add, reads=[ga, gb], writes=[g.mix[n]])
            for n in range(16):
                pss = projm(wo[l], n, 16, groups, [g.mix for g in groups])
                for gi, g in enumerate(groups):
                    P.copy("act", g.m2[n][:, 0:g.N], pss[gi][:, 0:g.N], reads=[pss[gi]], writes=[g.m2[n]])
            for g in groups:
                norm_post_add(g, l, V_LN2, g.m2)
        for g in groups:
            norm_pre(g, l, V_LN3)
        with P.scope() as alloc:
            for g in groups:
                g.act = [alloc(f"{g.name}_act{k}", [128, g.N], BF16) for k in range(NFF)]
                g.f = [alloc(f"{g.name}_f{k}", [128, g.N], F32) for k in range(16)]
                g.ext = [alloc(f"{g.name}_ext{k}", [128, g.N + 2], F32) for k in range(2)]
            w44[:] = [alloc(f"w44_{i}", [128, 44, 128], BF16) for i in range(2)]
            for j in range(NFF):
                psU = projm(wup[l], j, 16, groups, hin)
                psG = projm(wup[l], NFF + j, 16, groups, hin)
                for gi, g in enumerate(groups):
                    N = g.N
                    ext = g.ext[j % 2]
                    P.copy("act", ext[:, 2:N + 2], psU[gi][:, 0:N], reads=[psU[gi]], writes=[ext])
                    P.copy("dve", ext[:, 0:2], g.convp[l][:, j, :], reads=[g.convp[l], ext], writes=[ext])
                    P.copy("dve", g.convp[l][:, j, :], ext[:, N:N + 2], reads=[ext, g.convp[l]], writes=[g.convp[l]])
                    c = tmp(g)
                    cw = lambda jj: vec[l][:, V_CW + jj * NFF + j:V_CW + jj * NFF + j + 1]
                    P.ts("dve", c[:, 0:N], ext[:, 0:N], cw(0), vec[l][:, V_CB + j:V_CB + j + 1], ALU.mult, ALU.add,
                         reads=[ext, vec[l]], writes=[c])
                    P.stt("dve", c[:, 0:N], ext[:, 1:N + 1], cw(1), c[:, 0:N], ALU.mult, ALU.add, reads=[ext, vec[l], c], writes=[c])
                    P.stt("dve", c[:, 0:N], ext[:, 2:N + 2], cw(2), c[:, 0:N], ALU.mult, ALU.add, reads=[ext, vec[l], c], writes=[c])
                    p2 = tmp(g)
                    P.act(p2[:, 0:N], c[:, 0:N], AF.Square, reads=[c], writes=[p2])
                    P.ts("dve", p2[:, 0:N], p2[:, 0:N], 0.044715, 1.0, ALU.mult, ALU.add, reads=[p2], writes=[p2])
                    P.tt("dve", p2[:, 0:N], p2[:, 0:N], c[:, 0:N], ALU.mult, reads=[p2, c], writes=[p2])
                    P.act(p2[:, 0:N], p2[:, 0:N], AF.Sigmoid, reads=[p2], writes=[p2], scale=GELU_K)
                    P.tt("dve", p2[:, 0:N], p2[:, 0:N], c[:, 0:N], ALU.mult, reads=[p2, c], writes=[p2])
                    P.tt("dve", g.act[j][:, 0:N], p2[:, 0:N], psG[gi][:, 0:N], ALU.mult, reads=[p2, psG[gi]], writes=[g.act[j]])
            for n in range(16):
                pss = projm(wdn[l], n, 44, groups, [g.act for g in groups])
                for gi, g in enumerate(groups):
                    P.copy("act", g.f[n][:, 0:g.N], pss[gi][:, 0:g.N], reads=[pss[gi]], writes=[g.f[n]])
            for g in groups:
                norm_post_add(g, l, V_LN4, g.f)

    def rwkv_pre(l, groups, hin):
        for ch in (24, 25):
            pss = projm(win[l], ch, 16, groups, hin)
            for gi, g in enumerate(groups):
                N = g.N
                xs = tmp(g)
                shiftmix(g, l, ch, pss[gi], xs, g.uext[ch % 2])
                if ch == 24:
                    P.act(g.LA[0:64, 0:N], xs[0:64, 0:N], AF.Tanh, reads=[xs], writes=[g.LA])
                    P.copy("dve", g.LA[64:128, 0:N], xs[64:128, 0:N], reads=[xs, g.LA], writes=[g.LA])
                else:
                    P.act(g.LG[:, 0:N], xs[:, 0:N], AF.Sigmoid, reads=[xs], writes=[g.LG])
        for hp in range(8):
            for which, ch in (("xr", hp), ("xk", 8 + hp), ("xv", 16 + hp)):
                pss = projm(win[l], ch, 16, groups, hin)
                for gi, g in enumerate(groups):
                    shiftmix(g, l, ch, pss[gi], getattr(g, which), g.uext[ch % 2])
            for g in groups:
                rwkv_prep(l, g, hp)

    def rwkv_prep(l, g, hp):
        N, C, NCH = g.N, g.C, g.NCH
        cs = slice(hp * 128, (hp + 1) * 128)
        cg = slice(1024 + hp * 128, 1024 + (hp + 1) * 128)
        v = vec[l]
        col = lambda base: v[:, base + hp:base + hp + 1]
        wk = g.wk
        ps_d, ps_a, ps_g = P.psum(), P.psum(), P.psum()
        P.mm(ps_d, ps_d[:, 0:N], lora[l][0:64, cs], g.LA[0:64, 0:N], True, True, reads=[lora[l], g.LA])
        P.mm(ps_a, ps_a[:, 0:N], lora[l][64:128, cs], g.LA[64:128, 0:N], True, True, reads=[lora[l], g.LA])
        P.mm(ps_g, ps_g[:, 0:N], lora[l][:, cg], g.LG[:, 0:N], True, True, reads=[lora[l], g.LG])
        sg, cum, epos, eneg, eprev, a = wk[0], wk[1], wk[2], wk[3], wk[4], wk[5]
        P.act(sg[:, 0:N], ps_d[:, 0:N], AF.Sigmoid, reads=[ps_d, v], writes=[sg], bias=col(V_DB))
        P.act(a[:, 0:N], ps_a[:, 0:N], AF.Sigmoid, reads=[ps_a, v], writes=[a], bias=col(V_AB))
        P.copy("act", g.G[hp][:, 0:N], ps_g[:, 0:N], reads=[ps_g], writes=[g.G[hp]])
        rm = cst[:, CST_RM:CST_RM + N]
        P.op("dve", lambda e: e.tensor_tensor_scan(cum[:, 0:N], rm, sg[:, 0:N], 0.0, ALU.mult, ALU.add),
             reads=[cst, sg], writes=[cum])
        P.act(epos[:, 0:N], cum[:, 0:N], AF.Exp, reads=[cum], writes=[epos], scale=-DEC_K)
        P.act(eneg[:, 0:N], cum[:, 0:N], AF.Exp, reads=[cum], writes=[eneg], scale=DEC_K)
        P.tt("dve", eprev[:, 0:N], cum[:, 0:N], sg[:, 0:N], ALU.subtract, reads=[cum, sg], writes=[eprev])
        P.act(eprev[:, 0:N], eprev[:, 0:N], AF.Exp, reads=[eprev], writes=[eprev], scale=-DEC_K)
        P.copy("dve", g.gC[:, hp, :], epos[:, 0:N].rearrange("p (c t) -> p c t", t=C)[:, :, C - 1], reads=[epos], writes=[g.gC])
        kk0, sq, kap = wk[6], wk[7], wk[8]
        P.ts("dve", kk0[:, 0:N], g.xk[:, 0:N], col(V_KK), None, ALU.mult, None, reads=[g.xk, v], writes=[kk0])
        P.act(sq[:, 0:N], kk0[:, 0:N], AF.Square, reads=[kk0], writes=[sq])
        ps_s = P.psum()
        P.mm(ps_s, ps_s[:, 0:N], bones[:, :], sq[:, 0:N], True, True, reads=[bones, sq])
        P.ts("dve", sq[:, 0:N], ps_s[:, 0:N], 1e-24, None, ALU.max, None, reads=[ps_s], writes=[sq])
        P.op("dve", lambda e: e.reciprocal(sq[:, 0:N], sq[:, 0:N]), reads=[sq], writes=[sq])
        P.act(sq[:, 0:N], sq[:, 0:N], AF.Sqrt, reads=[sq], writes=[sq])
        P.tt("dve", kap[:, 0:N], kk0[:, 0:N], sq[:, 0:N], ALU.mult, reads=[kk0, sq], writes=[kap])
        t1, kpr = wk[9], wk[6]
        P.ts("dve", t1[:, 0:N], a[:, 0:N], -1.0, col(V_KA), ALU.add, ALU.mult, reads=[a, v], writes=[t1])
        P.stt("dve", kpr[:, 0:N], t1[:, 0:N], 1.0, g.xk[:, 0:N], ALU.add, ALU.mult, reads=[t1, g.xk], writes=[kpr])
        bb = wk[7]
        P.tt("dve", bb[:, 0:N], kap[:, 0:N], a[:, 0:N], ALU.mult, reads=[kap, a], writes=[bb])
        rk = wk[9]
        P.stt("dve", rk[:, 0:N], g.xr[:, 0:N], col(V_RK), kpr[:, 0:N], ALU.mult, ALU.mult, reads=[g.xr, v, kpr], writes=[rk])
        ps_b = P.psum()
        P.mm(ps_b, ps_b[:, 0:N], bones[:, :], rk[:, 0:N], True, True, reads=[bones, rk])
        P.tt("dve", g.bonus[hp][:, 0:N], ps_b[:, 0:N], g.xv[:, 0:N], ALU.mult, reads=[ps_b, g.xv], writes=[g.bonus[hp]])
        r3 = lambda b: b[:, 0:N].rearrange("p (c t) -> p c t", t=C)
        P.tt("dve", g.KR[hp][:, :, 0:C], r3(kap), r3(eprev), ALU.mult, reads=[kap, eprev], writes=[g.KR[hp]])
        P.tt("dve", g.KR[hp][:, :, C:2 * C], r3(g.xr), r3(epos), ALU.mult, reads=[g.xr, epos, g.KR[hp]], writes=[g.KR[hp]])
        P.tt("dve", g.KT[hp][:, 0:N], kpr[:, 0:N], eneg[:, 0:N], ALU.mult, reads=[kpr, eneg], writes=[g.KT[hp]])
        P.tt("dve", g.BT[hp][:, 0:N], bb[:, 0:N], eneg[:, 0:N], ALU.mult, reads=[bb, eneg], writes=[g.BT[hp]])
        P.copy("act", g.VF[hp][:, 0:N], g.xv[:, 0:N], reads=[g.xv], writes=[g.VF[hp]])

    def rwkv_scan(l, g):
        N, C, NCH, nlev = g.N, g.C, g.NCH, g.nlev
        HB2 = min(8, 512 // (2 * C))
        if C == 64:
            mab = cst[0:C, CST_MAB:CST_MAB + 512].rearrange("p (h w) -> p h w", w=2 * C)
            ml = cst[0:C, CST_ML:CST_ML + 512].rearrange("p (h w) -> p h w", w=C)
            i8 = cst[0:C, CST_I8:CST_I8 + 512].rearrange("p (h w) -> p h w", w=C)
        else:
            mab = cst[0:C, CST_MAB4:CST_MAB4 + 64].rearrange("p (h w) -> p h w", w=2 * C)
            ml = cst[0:C, CST_ML4:CST_ML4 + 32].rearrange("p (h w) -> p h w", w=C)
            i8 = cst[0:C, CST_I84:CST_I84 + 32].rearrange("p (h w) -> p h w", w=C)
        r3 = lambda ps, w, n=8: ps[0:C, 0:n * w].rearrange("p (h w) -> p h w", w=w)
        Zs_e = g.Zs_e[l]
        Zm_e = g.Zm_e[l]
        Zmt = g.Zm[l]

        def indep(c):
            S = g.cb[c % 2]
            cs = slice(c * C, (c + 1) * C)
            for (src, dst) in ((g.VF, S.Vtok), (g.KT, S.Ktok), (g.BT, S.Btok)):
                for hs in (0, 4):
                    ps = P.psum()
                    for j in range(4):
                        P.mm(ps, ps[0:C, j * 128:(j + 1) * 128], src[hs + j][:, cs], identb(), True, True, reads=[src[hs + j], cstb])
                    P.copy("act", dst[0:C, hs * 128:(hs + 4) * 128], ps[0:C, 0:512], reads=[ps], writes=[dst])
                yield
            for e in (0, 1):
                rows = slice(e * 64, (e + 1) * 64)
                for (lh, dst) in ((g.KT, S.AkBk[e]), (g.BT, S.AbBb[e])):
                    for hs in range(0, 8, HB2):
                        ps = P.psum()
                        for j in range(HB2):
                            hp = hs + j
                            P.mm(ps, ps[0:C, j * 2 * C:(j + 1) * 2 * C], lh[hp][rows, cs], g.KR[hp][rows, c, :], True, True,
                                 reads=[lh[hp], g.KR[hp]])
                        P.tt("dve", dst[0:C, hs:hs + HB2, :], r3(ps, 2 * C, HB2), mab[:, 0:HB2, :], ALU.mult, reads=[ps, cst], writes=[dst])
                ps = P.psum()
                for hp in range(8):
                    P.mm(ps, ps[0:C, hp * C:(hp + 1) * C], g.KR[hp][rows, c, 0:C], g.BT[hp][rows, cs], True, True,
                         reads=[g.KR[hp], g.BT[hp]])
                P.tt("dve", S.Pm[e][0][0:C, :, :], r3(ps, C), ml, ALU.mult, reads=[ps, cst], writes=[S.Pm[e][0]])
                yield
                P.tt("dve", S.Nm[e][0:C, :, :], i8, S.AbBb[e][0:C, :, 0:C], ALU.subtract, reads=[cst, S.AbBb[e]], writes=[S.Nm[e]])
                P.copy("act", S.Ns[e][0:C, :, :], S.Nm[e][0:C, :, :], reads=[S.Nm[e]], writes=[S.Ns[e]])
                Pk = lambda k, hp: S.Pm[e][k % 2][0:C, hp, :]
                Pb_ = lambda k: S.Pm[e][k % 2]
                Qk = lambda k, hp: (S.AbBb[e][0:C, hp, 0:C] if k == 0 else S.Qm[e][k % 2][0:C, hp, :])
                Qb_ = lambda k: (S.AbBb[e] if k == 0 else S.Qm[e][k % 2])
                for k in range(nlev):
                    if k >= 1:
                        ps = P.psum()
                        for hp in range(8):
                            P.mm(ps, ps[0:C, hp * C:(hp + 1) * C], Pk(k, hp), S.Ns[e][0:C, hp, :], True, True, reads=[Pb_(k), S.Ns[e]])
                        P.tt("dve", S.Nm[e][0:C, :, :], r3(ps, C), S.Nm[e][0:C, :, :], ALU.add, reads=[ps, S.Nm[e]], writes=[S.Nm[e]])
                        P.copy("act", S.Ns[e][0:C, :, :], S.Nm[e][0:C, :, :], reads=[S.Nm[e]], writes=[S.Ns[e]])
                    if k < nlev - 1:
                        ps = P.psum()
                        for hp in range(8):
                            P.mm(ps, ps[0:C, hp * C:(hp + 1) * C], Qk(k, hp), Pk(k, hp), True, True, reads=[Qb_(k), Pb_(k)])
                        P.copy("act", S.Pm[e][(k + 1) % 2][0:C, :, :], r3(ps, C), reads=[ps], writes=[S.Pm[e][(k + 1) % 2]])
                        if k + 1 < nlev - 1:
                            ps = P.psum()
                            for hp in range(8):
                                P.mm(ps, ps[0:C, hp * C:(hp + 1) * C], Pk(k, hp), Qk(k, hp), True, True, reads=[Qb_(k), Pb_(k)])
                            P.copy("dve", S.Qm[e][(k + 1) % 2][0:C, :, :], r3(ps, C), reads=[ps], writes=[S.Qm[e][(k + 1) % 2]])
                    yield

        def dep(c, filler):
            S = g.cb[c % 2]
            cs = slice(c * C, (c + 1) * C)

            def fill(n=3):
                for _ in range(n):
                    try:
                        next(filler)
                    except StopIteration:
                        break
            for e in (0, 1):
                rows = slice(e * 64, (e + 1) * 64)
                vcol = lambda hp: slice(hp * 128 + e * 64, hp * 128 + e * 64 + 64)
                ps = P.psum()
                for hp in range(8):
                    o = ps[0:C, hp * 64:(hp + 1) * 64]
                    P.mm(ps, o, g.KR[hp][:, c, 0:C], Zs_e[e][:, hp, :], True, False, reads=[g.KR[hp], Zs_e[e]])
                    P.mm(ps, o, S.AkBk[e][0:C, hp, 0:C], S.Vtok[0:C, vcol(hp)], False, True, reads=[S.AkBk[e], S.Vtok])
                P.copy("act", S.RHS[e][0:C, :, :], r3(ps, 64), reads=[ps], writes=[S.RHS[e]])
                ps = P.psum()
                for hp in range(8):
                    P.mm(ps, ps[0:C, hp * 64:(hp + 1) * 64], S.Ns[e][0:C, hp, :], S.RHS[e][0:C, hp, :], True, True, reads=[S.Ns[e], S.RHS[e]])
                P.ts("dve", S.Un[e][0:C, :, :], r3(ps, 64), -1.0, None, ALU.mult, None, reads=[ps], writes=[S.Un[e]])
                ps = P.psum()
                for hp in range(8):
                    o = ps[0:64, hp * C:(hp + 1) * C]
                    P.mm(ps, o, Zs_e[e][:, hp, :], g.KR[hp][:, c, C:2 * C], True, False, reads=[Zs_e[e], g.KR[hp]])
                    P.mm(ps, o, S.Vtok[0:C, vcol(hp)], S.AkBk[e][0:C, hp, C:2 * C], False, False, reads=[S.Vtok, S.AkBk[e]])
                    P.mm(ps, o, S.Un[e][0:C, hp, :], S.AbBb[e][0:C, hp, C:2 * C], False, True, reads=[S.Un[e], S.AbBb[e]])
                P.copy("act", g.oT[rows, :, cs], ps[0:64, 0:8 * C].rearrange("p (h w) -> p h w", w=C), reads=[ps], writes=[g.oT])
                ps = P.psum()
                for hp in range(8):
                    o = ps[0:64, hp * 64:(hp + 1) * 64]
                    P.mm(ps, o, S.Ktok[0:C, vcol(hp)], S.Vtok[0:C, vcol(hp)], True, False, reads=[S.Ktok, S.Vtok])
                    P.mm(ps, o, S.Btok[0:C, vcol(hp)], S.Un[e][0:C, hp, :], False, True, reads=[S.Btok, S.Un[e]])
                z3 = ps[0:64, 0:512].rearrange("p (h w) -> p h w", w=64)
                P.tt("dve", Zmt[rows, :, :], z3, Zmt[rows, :, :], ALU.add, reads=[ps, Zm_e[e]], writes=[Zm_e[e]])
                P.tt("dve", Zmt[rows, :, :], Zmt[rows, :, :], g.gC[rows, :, c:c + 1].to_broadcast([64, 8, 64]), ALU.mult,
                     reads=[Zm_e[e], g.gC], writes=[Zm_e[e]])
                P.copy("act", Zs_e[e][rows, :, :], Zmt[rows, :, :], reads=[Zm_e[e]], writes=[Zs_e[e]])
                fill(6)

        for c in range(NCH):
            for _ in indep(c):
                pass
            dep(c, iter(()))

    def rwkv_out(l, g):
        N = g.N
        if "dbg_o" in d and g.N == NT and l == 0:
            P.dma("sp", "dbg0", [(d["dbg_o"][:, :, :], g.oT[:, :, :])], reads=[g.oT], writes=[d["dbg_o"]])
            P.dma("sp", "dbg1", [(d["dbg_kr"][hp], g.KR[hp][:, :, :].rearrange("p c w -> p (c w)")) for hp in range(8)], reads=g.KR, writes=[d["dbg_kr"]])
            P.dma("sp", "dbg2", [(d["dbg_x"][0], g.xr[:, :]), (d["dbg_x"][1], g.xk[:, :]), (d["dbg_x"][2], g.xv[:, :])], reads=[g.xr, g.xk, g.xv], writes=[d["dbg_x"]])
        v = vec[l]
        wk = g.wk
        for hp in range(8):
            col = lambda base: v[:, base + hp:base + hp + 1]
            o = g.oT[:, hp, 0:N]
            ps = P.psum()
            P.mm(ps, ps[:, 0:N], bones[:, :], o, True, True, reads=[bones, g.oT])
            cen, sq = wk[0], wk[1]
            P.stt("dve", cen[:, 0:N], ps[:, 0:N], -1.0 / 64, o, ALU.mult, ALU.add, reads=[ps, g.oT], writes=[cen])
            P.act(sq[:, 0:N], cen[:, 0:N], AF.Square, reads=[cen], writes=[sq])
            ps2 = P.psum()
            P.mm(ps2, ps2[:, 0:N], bones[:, :], sq[:, 0:N], True, True, reads=[bones, sq])
            P.ts("dve", sq[:, 0:N], ps2[:, 0:N], 1.0 / 64, GN_EPS, ALU.mult, ALU.add, reads=[ps2], writes=[sq])
            P.op("dve", lambda e: e.reciprocal(sq[:, 0:N], sq[:, 0:N]), reads=[sq], writes=[sq])
            P.act(sq[:, 0:N], sq[:, 0:N], AF.Sqrt, reads=[sq], writes=[sq])
            P.tt("dve", cen[:, 0:N], cen[:, 0:N], sq[:, 0:N], ALU.mult, reads=[cen, sq], writes=[cen])
            P.ts("dve", cen[:, 0:N], cen[:, 0:N], col(V_GW), col(V_GB), ALU.mult, ALU.add, reads=[cen, v], writes=[cen])
            P.tt("dve", cen[:, 0:N], cen[:, 0:N], g.bonus[hp][:, 0:N], ALU.add, reads=[cen, g.bonus[hp]], writes=[cen])
            P.tt("dve", g.ya[hp][:, 0:N], cen[:, 0:N], g.G[hp][:, 0:N], ALU.mult, reads=[cen, g.G[hp]], writes=[g.ya[hp]])
        if "dbg_o" in d and g.N == NT and l == 0:
            P.dma("sp", "dbg3", [(d["dbg_ya"][hp], g.ya[hp][:, :]) for hp in range(8)], reads=g.ya, writes=[d["dbg_ya"]])

    def moba_alloc():
        M = Group()
        M.F = [P.sbuf(f"Ftab{h}", [128, 256], F32) for h in range(8)]
        M.B31x = P.sbuf("B31x", [128, 8, 8], F32)
        M.kms = [P.sbuf(f"kms{l}", [128, 8, max(T // 256, 1)], BF16) for l in range(L)]
        M.kmf = [P.sbuf(f"kmf{l}", [128, 8, max(T // 256, 1)], F32) for l in range(L)]
        return M

    def sample_alloc():
        SM = Group()
        SM.Idx = P.sbuf("Idx", [128, cfg.n_pages], I32)
        SM.BL = P.sbuf("BiasLast", [128, 8, 4], F32)
        SM.BO = P.sbuf("BiasOwn", [4, 8, 4], F32)
        SM.ones_b = P.sbuf("ones_b", [128, 128], BF16)
        return SM

    def moba_setup(salloc, M):
        relb = salloc("relb", [64, 8], F32)
        oh = salloc("oh", [64, 512], F32)
        P.dma("sp", "ld_c2", [(relb[:, :], d["relb_d"][:, :]), (oh[:, :], d["onehot_d"][:, :])], reads=[d["relb_d"]], writes=[relb, oh])
        M.relb = relb
        if True:
            alloc = salloc
            rep = [alloc(f"rep{i}", [64, 128], F32) for i in range(2)]
            yr = [alloc(f"yr{i}", [128, 384], F32) for i in range(2)]
            for h in range(8):
                r_ = rep[h % 2]
                P.copy("dve", r_[:, :], relb[:, h:h + 1].to_broadcast([64, 128]), reads=[relb], writes=[r_])
                ps = P.psum()
                P.mm(ps, ps[:, 0:384], r_[:, :], oh[:, 0:384], True, True, reads=[r_, oh])
                y_ = yr[h % 2]
                P.copy("act", y_[:, :], ps[:, 0:384], reads=[ps], writes=[y_])
                tz = d["tz"][h]
                P.dma("sp", f"tzw{h}", [(tz[:, :], y_[:, :])], reads=[y_], writes=[tz])
                src = bass.AP(tz.h, 127, [[383, 128], [1, 256]])
                P.dma("sp", f"tzr{h}", [(M.F[h][:, :], src)], reads=[tz], writes=[M.F[h]])
            ps = P.psum()
            P.mm(ps, ps[:, 0:8], oh[:, 384:512], relb[:, 0:8], True, True, reads=[oh, relb])
            b31 = alloc("b31", [128, 8], F32)
            P.copy("act", b31[:, :], ps[:, 0:8], reads=[ps], writes=[b31])
            P.copy("dve", M.B31x[:, :, :], b31[:, :].unsqueeze(2).to_broadcast([128, 8, 8]), reads=[b31], writes=[M.B31x])
        return M

    def moba(l, groups, hin, tile_i, alloc):
        gp = groups[0]
        gsm = groups[1] if len(groups) > 1 else None
        N = NT
        t0 = tile_i * NT
        M = MB
        QT = [alloc(f"QT{h}", [128, N], BF16) for h in range(8)]
        TK = t0 + NT
        KA = [alloc(f"KA{h}", [128, TK], BF16) for h in range(8)]
        VA = alloc("VA", [128, TK // 128, 1024], BF16)
        kf = [alloc(f"kf{i}", [128, N], F32) for i in range(2)]
        vf = [alloc(f"vf{i}", [128, 1024], F32) for i in range(NS)]
        nbuf = 2 if TK <= 1024 else 1
        Pb = [alloc(f"Pb{i}", [128, TK], BF16) for i in range(nbuf)] * (3 - nbuf)
        PT = [alloc(f"PT{i}", [128, TK // 128, 128], BF16) for i in range(nbuf)] * (3 - nbuf)
        wvb[:] = [alloc("wvb", [128, 16, 256], BF16)]
        Gs = alloc("Gs", [128, 8, 8], F32)
        top8 = alloc("top8", [128, 8, 8], F32)
        SB = alloc("SB", [128, 8, 8], F32)
        SBb = alloc("SBb", [128, 8, 8], F32)
        dcol = [alloc(f"dcol{i}", [128, 20], F32) for i in range(2)]
        dsum = [alloc(f"dsum{i}", [128, 1], F32) for i in range(2)]
        Dg = [alloc(f"Dg{i}", [128, 128], BF16) for i in range(2)]
        tS = [alloc(f"tS{i}", [128, 128], F32) for i in range(2)]
        if gsm is not None:
            gsm.QT = [alloc(f"sQT{h}", [128, 4], BF16) for h in range(8)]
            gsm.knf = alloc("s_knf", [128, 8, 4], F32)
            gsm.knb = alloc("s_knb", [128, 8, 4], BF16)
            gsm.vn = alloc("s_vn", [4, 1024], F32)
            gsm.vnb = alloc("s_vnb", [4, 1024], BF16)
        if getattr(cfg, "moba_stop", 9) < 0.2:
            for k in range(8):
                P.op("dve", lambda e: e.memset(gp.yb[k][:, :], 0.0), writes=[gp.yb[k]])
            return
        if t0 > 0:
            P.dma("pool", "kh", [(KA[h][:, 0:t0], kTo.t[l, h * 128:(h + 1) * 128, 0:t0]) for h in range(8)], reads=[kTo], writes=KA)
            P.dma("pool", "vh", [(VA[:, 0:t0 // 128, :], vo.t[l, 0:t0, :].rearrange("(k p) f -> p k f", p=128))], reads=[vo], writes=[VA])
        for h in range(8):
            pss = projm(win[l], 26 + h, 16, groups, hin)
            P.copy("act", QT[h][:, 0:N], pss[0][:, 0:N], reads=[pss[0]], writes=[QT[h]])
            if gsm is not None:
                P.copy("act", gsm.QT[h][:, 0:4], pss[1][:, 0:4], reads=[pss[1]], writes=[gsm.QT[h]])
        for h in range(8):
            pss = projm(win[l], 34 + h, 16, groups, hin)
            k_ = kf[h % 2]
            P.copy("act", k_[:, 0:N], pss[0][:, 0:N], reads=[pss[0]], writes=[k_])
            P.copy("dve", KA[h][:, t0:t0 + N], k_[:, 0:N], reads=[k_], writes=[KA[h]])
            P.dma("sp", f"st_k{h % 2}", [(kTo.t[l, h * 128:(h + 1) * 128, t0:t0 + N], k_[:, 0:N])], reads=[k_], writes=[kTo])
            for b in range(N // 256):
                bi = t0 // 256 + b
                P.op("dve", lambda e: e.reduce_sum(M.kmf[l][:, h, bi:bi + 1], k_[:, b * 256:(b + 1) * 256], axis=AX.X),
                     reads=[k_], writes=[M.kmf[l]])
                P.copy("dve", M.kms[l][:, h, bi:bi + 1], M.kmf[l][:, h, bi:bi + 1], reads=[M.kmf[l]], writes=[M.kms[l]])
            if gsm is not None:
                P.copy("act", gsm.knf[:, h, :], pss[1][:, 0:4], reads=[pss[1]], writes=[gsm.knf])
                P.copy("dve", gsm.knb[:, h, :], gsm.knf[:, h, :], reads=[gsm.knf], writes=[gsm.knb])
        if getattr(cfg, "moba_stop", 9) < 0.7:
            for k in range(8):
                P.op("dve", lambda e: e.memset(gp.yb[k][:, :], 0.0), writes=[gp.yb[k]])
            return
        for gc in range(4):
            w = getw(wvr[l], gc, "v")
            gcs = slice(gc * 256, (gc + 1) * 256)
            for sub in range(NS):
                ps = P.psum()
                for k in range(16):
                    P.mm(ps, ps[:, 0:256], gp.h[k][:, sub * 128:(sub + 1) * 128], w[:, k, :], k == 0, k == 15, reads=[gp.h[k], w])
                P.copy("act", vf[sub][:, gcs], ps[:, 0:256], reads=[ps], writes=[vf[sub]])
                P.copy("dve", VA[:, t0 // 128 + sub, gcs], vf[sub][:, gcs], reads=[vf[sub]], writes=[VA])
            if gsm is not None:
                ps = P.psum()
                for k in range(16):
                    P.mm(ps, ps[0:4, 0:256], gsm.h[k][:, 0:4], w[:, k, :], k == 0, k == 15, reads=[gsm.h[k], w])
                P.copy("act", gsm.vn[0:4, gcs], ps[0:4, 0:256], reads=[ps], writes=[gsm.vn])
                P.copy("dve", gsm.vnb[0:4, gcs], gsm.vn[0:4, gcs], reads=[gsm.vn], writes=[gsm.vnb])
        for sub in range(NS):
            P.dma("sp", f"st_v{sub}", [(vo.t[l, t0 + sub * 128:t0 + (sub + 1) * 128, :], vf[sub][:, :])], reads=[vf[sub]], writes=[vo])
        it = 0
        if getattr(cfg, "moba_stop", 9) < 2:
            for k in range(8):
                P.op("dve", lambda e: e.memset(gp.yb[k][:, :], 0.0), writes=[gp.yb[k]])
            return
        for qs in range(NS):
            qsg = t0 // 128 + qs
            own = qsg // 2
            qcols = slice(qs * 128, (qs + 1) * 128)
            if own > 3:
                psG = P.psum()
                for h in range(8):
                    P.mm(psG, psG[:, h * 8:h * 8 + own], QT[h][:, qcols], M.kms[l][:, h, 0:own], True, True, reads=[QT[h], M.kms[l]])
                P.op("dve", lambda e: e.memset(Gs[:, :, :], -1e30), writes=[Gs])
                P.copy("dve", Gs[:, :, 0:own], psG[:, 0:64].rearrange("p (h w) -> p h w", w=8)[:, :, 0:own], reads=[psG, Gs], writes=[Gs])
                for h in range(8):
                    P.op("dve", lambda e: e.max(top8[:, h, :], Gs[:, h, :]), reads=[Gs], writes=[top8])
                for h in range(8):
                    P.ts("dve", SB[:, h, :], Gs[:, h, :], top8[:, h, 2:3], NEG, ALU.is_lt, ALU.mult, reads=[Gs, top8, SB], writes=[SB])
            else:
                P.op("dve", lambda e: e.memset(SB[:, :, :], 0.0), writes=[SB])
            P.tt("dve", SBb[:, :, :], SB[:, :, :], M.B31x[:, :, :], ALU.add, reads=[SB, M.B31x], writes=[SBb])
            nkt = qsg + 1
            kend = nkt * 128
            for h in range(8):
                pb, pt, dc, ds, dg = Pb[it % 2], PT[it % 2], dcol[it % 2], dsum[it % 2], Dg[it % 2]
                it += 1
                banks = []
                for kg in range((kend + 511) // 512):
                    w_ = min(512, kend - kg * 512)
                    ps = P.psum()
                    P.mm(ps, ps[:, 0:w_], QT[h][:, qcols], KA[h][:, kg * 512:kg * 512 + w_], True, True, reads=[QT[h], KA[h]])
                    banks.append(ps)
                pcol = lambda kt: (banks[kt // 4], (kt % 4) * 128)
                nreg = 0
                P.op("dve", lambda e: e.memset(dc[:, :], 0.0), writes=[dc])
                kt = 0
                while kt < qsg - 1:
                    n = kt // 2
                    span = 2 if (kt % 2 == 0 and kt + 1 < qsg - 1) else 1
                    bk, c0 = pcol(kt)
                    P.act(pb[:, kt * 128:(kt + span) * 128], bk[:, c0:c0 + span * 128], AF.Exp, reads=[bk, SBb], writes=[pb, dc],
                          scale=SCALE, bias=SBb[:, h, n:n + 1], accum_out=dc[:, nreg:nreg + 1])
                    nreg += 1
                    kt += span
                for kt in (qsg - 1, qsg):
                    if kt < 0:
                        continue
                    n = kt // 2
                    bk, c0 = pcol(kt)
                    t_ = tS[nreg % 2]
                    fc = 0 if kt == qsg - 1 else 128
                    P.act(t_[:, :], bk[:, c0:c0 + 128], AF.Copy, reads=[bk], writes=[t_], scale=SCALE)
                    P.tt("dve", t_[:, :], t_[:, :], M.F[h][:, fc:fc + 128], ALU.add, reads=[t_, M.F[h]], writes=[t_])
                    if n < own:
                        P.act(pb[:, kt * 128:(kt + 1) * 128], t_[:, :], AF.Exp, reads=[t_, SB], writes=[pb, dc],
                              bias=SB[:, h, n:n + 1], accum_out=dc[:, nreg:nreg + 1])
                    else:
                        P.act(pb[:, kt * 128:(kt + 1) * 128], t_[:, :], AF.Exp, reads=[t_], writes=[pb, dc], accum_out=dc[:, nreg:nreg + 1])
                    nreg += 1
                P.op("dve", lambda e: e.reduce_sum(ds[:, 0:1], dc[:, 0:nreg], axis=AX.X), reads=[dc], writes=[ds])
                P.op("dve", lambda e: e.reciprocal(ds[:, 0:1], ds[:, 0:1]), reads=[ds], writes=[ds])
                P.ts("dve", dg[:, :], cst[:, CST_ID:CST_ID + 128], ds[:, 0:1], None, ALU.mult, None, reads=[cst, ds], writes=[dg])
                for kg in range(0, nkt, 4):
                    kk_ = min(4, nkt - kg)
                    ps = P.psum()
                    for j in range(kk_):
                        P.mm(ps, ps[:, j * 128:(j + 1) * 128], pb[:, (kg + j) * 128:(kg + j + 1) * 128], dg[:, :], True, True, reads=[pb, dg])
                    P.copy("act" if (kg // 4) % 2 == 0 else "dve", pt[:, kg:kg + kk_, :], ps[:, 0:kk_ * 128].rearrange("p (k w) -> p k w", w=128),
                           reads=[ps], writes=[pt])
                psy = P.psum()
                for kt in range(nkt):
                    P.mm(psy, psy[:, 0:128], VA[:, kt, h * 128:(h + 1) * 128], pt[:, kt, :], kt == 0, kt == nkt - 1, reads=[VA, pt])
                P.copy("act", gp.yb[h][:, qcols], psy[:, 0:128], reads=[psy], writes=[gp.yb[h]])
        if gsm is not None:
            moba_sample(l, gsm, alloc)


    def sample_setup(salloc, SM):
        npg = cfg.n_pages
        ptb = salloc("ptb", [128, npg], I32)
        ohs = salloc("ohs", [64, 4, 132], F32)
        P.dma("sp", "ld_pt", [(ptb[:, :], d["ptr"][:, :]), (ohs[:, :, :], d["ohs"][:, :, :])], reads=[d["ptr"]], writes=[ptb, ohs])
        ptf = salloc("ptf", [128, npg], F32)
        P.copy("dve", ptf[:, :], ptb[:, :], reads=[ptb], writes=[ptf])
        P.ts("dve", ptf[:, :], ptf[:, :], 128.0, cst[:, CST_PIDX:CST_PIDX + 1], ALU.mult, ALU.add, reads=[ptf, cst], writes=[ptf])
        P.copy("dve", SM.Idx[:, :], ptf[:, :], reads=[ptf], writes=[SM.Idx])
        psL = P.psum()
        psO = P.psum()
        for qi in range(4):
            P.mm(psL, psL[:, qi * 8:(qi + 1) * 8], ohs[:, qi, 0:128], MB.relb[:, 0:8], True, True, reads=[ohs, MB.relb])
            P.mm(psO, psO[0:4, qi * 8:(qi + 1) * 8], ohs[:, qi, 128:132], MB.relb[:, 0:8], True, True, reads=[ohs, MB.relb])
        for qi in range(4):
            P.copy("act", SM.BL[:, :, qi], psL[:, qi * 8:(qi + 1) * 8], reads=[psL], writes=[SM.BL])
            P.copy("act", SM.BO[0:4, :, qi], psO[0:4, qi * 8:(qi + 1) * 8], reads=[psO], writes=[SM.BO])
        P.copy("dve", SM.ones_b[:, :], ones_f[:, :], reads=[ones_f], writes=[SM.ones_b])
        return SM

    def moba_sample(l, g, alloc):
        npg = cfg.n_pages
        NB = npg // 2
        SM = SMB
        Kp = [alloc(f"sKp{i}", [128, 1024], F32) for i in range(2)]
        Vp = [alloc(f"sVp{i}", [128, 1024], F32) for i in range(2)]
        Kb = [alloc(f"sKb{i}", [128, 1024], BF16) for i in range(2)]
        Vb = [alloc(f"sVb{i}", [128, 1024], BF16) for i in range(2)]
        KTp = [alloc(f"sKT{i}", [128, 8, 128], BF16) for i in range(2)]
        Ypart = alloc("sYp", [128, NB + 1, 32], F32)
        Dpart = alloc("sDp", [128, NB + 1, 32], F32)
        kmt = alloc("skmt", [128, 8, 2], F32)
        kmS = alloc("skmS", [128, 8, NB], F32)
        kmSb = alloc("skmSb", [128, 8, NB], BF16)
        tE = [alloc(f"stE{i}", [128, 8, 4], F32) for i in range(2)]
        Eb = [alloc(f"sEb{i}", [128, 32], BF16) for i in range(2)]
        ck = d[f"cache_k{l}"].t
        cv = d[f"cache_v{l}"].t
        psY = psD = None
        for i in range(npg):
            par = i % 2
            n = i // 2
            kp, vp, kb, vb, kt_, te, eb = Kp[par], Vp[par], Kb[par], Vb[par], KTp[par], tE[par], Eb[par]
            P.dma_gather(f"gk{par}", kp[:, :], ck, SM.Idx[:, i:i + 1], reads=[d[f"cache_k{l}"], SM.Idx], writes=[kp])
            P.dma_gather(f"gv{par}", vp[:, :], cv, SM.Idx[:, i:i + 1], reads=[d[f"cache_v{l}"], SM.Idx], writes=[vp])
            P.copy("dve", kb[:, :], kp[:, :], reads=[kp], writes=[kb])
            P.copy("act", vb[:, :], vp[:, :], reads=[vp], writes=[vb])
            for hg in (0, 4):
                ps = P.psum()
                for j in range(4):
                    P.mm(ps, ps[:, j * 128:(j + 1) * 128], kb[:, (hg + j) * 128:(hg + j + 1) * 128], identb(), True, True, reads=[kb, cstb])
                P.copy("act" if hg == 0 else "dve", kt_[:, hg:hg + 4, :], ps[:, 0:512].rearrange("p (h w) -> p h w", w=128), reads=[ps], writes=[kt_])
            P.op("dve", lambda e: e.reduce_sum(kmt[:, :, par], kt_[:, :, :], axis=AX.X), reads=[kt_], writes=[kmt])
            if par == 1:
                P.tt("dve", kmS[:, :, n], kmt[:, :, 0], kmt[:, :, 1], ALU.add, reads=[kmt], writes=[kmS])
            ps = P.psum()
            for h in range(8):
                P.mm(ps, ps[:, h * 4:(h + 1) * 4], kt_[:, h, :], g.QT[h][:, 0:4], True, True, reads=[kt_, g.QT[h]])
            bias3 = SM.BL[:, :, :] if i == npg - 1 else MB.B31x[:, :, 0:4]
            P.stt("dve", te[:, :, :], ps[:, 0:32].rearrange("p (h q) -> p h q", q=4), SCALE, bias3, ALU.mult, ALU.add,
                  reads=[ps, SM.BL, MB.B31x], writes=[te])
            P.act(eb[:, :], te[:, :, :].rearrange("p h q -> p (h q)"), AF.Exp, reads=[te], writes=[eb])
            psY = P.psum()
            for h in range(8):
                P.mm(psY, psY[:, h * 4:(h + 1) * 4], vb[:, h * 128:(h + 1) * 128], eb[:, h * 4:(h + 1) * 4], True, True, reads=[vb, eb])
            psD = P.psum()
            P.mm(psD, psD[:, 0:32], SM.ones_b[:, :], eb[:, :], True, True, reads=[SM.ones_b, eb])
            if par == 0:
                P.copy("act", Ypart[:, n, :], psY[:, 0:32], reads=[psY], writes=[Ypart])
                P.copy("act", Dpart[:, n, :], psD[:, 0:32], reads=[psD], writes=[Dpart])
            else:
                P.tt("dve", Ypart[:, n, :], psY[:, 0:32], Ypart[:, n, :], ALU.add, reads=[psY, Ypart], writes=[Ypart])
                P.tt("dve", Dpart[:, n, :], psD[:, 0:32], Dpart[:, n, :], ALU.add, reads=[psD, Dpart], writes=[Dpart])
        ps = P.psum()
        for h in range(8):
            P.mm(ps, ps[0:4, h * 4:(h + 1) * 4], g.knb[:, h, :], g.QT[h][:, 0:4], True, True, reads=[g.knb, g.QT[h]])
        te, eb = tE[0], Eb[0]
        P.stt("dve", te[0:4, :, :], ps[0:4, 0:32].rearrange("p (h q) -> p h q", q=4), SCALE, SM.BO[0:4, :, :], ALU.mult, ALU.add,
              reads=[ps, SM.BO], writes=[te])
        P.act(eb[0:4, :], te[0:4, :, :].rearrange("p h q -> p (h q)"), AF.Exp, reads=[te], writes=[eb])
        psY = P.psum()
        for h in range(8):
            P.mm(psY, psY[:, h * 4:(h + 1) * 4], g.vnb[0:4, h * 128:(h + 1) * 128], eb[0:4, h * 4:(h + 1) * 4], True, True, reads=[g.vnb, eb])
        psD = P.psum()
        P.mm(psD, psD[:, 0:32], SM.ones_b[0:4, :], eb[0:4, :], True, True, reads=[SM.ones_b, eb])
        P.copy("act", Ypart[:, NB, :], psY[:, 0:32], reads=[psY], writes=[Ypart])
        P.copy("act", Dpart[:, NB, :], psD[:, 0:32], reads=[psD], writes=[Dpart])
        sel = alloc("ssel", [4, 8, NB], F32)
        Gq = alloc("sGq", [4, 8, NB], F32)
        t8 = alloc("st8", [4, 8, 8], F32)
        P.copy("dve", kmSb[:, :, :], kmS[:, :, :], reads=[kmS], writes=[kmSb])
        if NB > 3:
            ps = P.psum()
            for h in range(8):
                P.mm(ps, ps[0:4, h * NB:(h + 1) * NB], g.QT[h][:, 0:4], kmSb[:, h, :], True, True, reads=[g.QT[h], kmSb])
            P.copy("dve", Gq[0:4, :, :], ps[0:4, 0:8 * NB].rearrange("p (h n) -> p h n", n=NB), reads=[ps], writes=[Gq])
            for h in range(8):
                P.op("dve", lambda e: e.max(t8[0:4, h, :], Gq[0:4, h, :]), reads=[Gq], writes=[t8])
            for h in range(8):
                P.ts("dve", sel[0:4, h, :], Gq[0:4, h, :], t8[0:4, h, 2:3], None, ALU.is_ge, None, reads=[Gq, t8, sel], writes=[sel])
        else:
            P.op("dve", lambda e: e.memset(sel[0:4, :, :], 1.0), writes=[sel])
        ybs = alloc("sybs", [128, 8, 4], F32)
        dbs = alloc("sdbs", [128, 8, 4], F32)
        tY = alloc("stY", [128, 8, NB], F32)
        for qi in range(4):
            psB = P.psum()
            P.mm(psB, psB[:, 0:8 * NB], cst[0:4, CST_OHQ + qi * 128:CST_OHQ + (qi + 1) * 128], sel[0:4, :, :].rearrange("p h n -> p (h n)"),
                 True, True, reads=[cst, sel])
            selb = psB[:, 0:8 * NB].rearrange("p (h n) -> p h n", n=NB)
            for (part, dst) in ((Ypart, ybs), (Dpart, dbs)):
                src = part[:, 0:NB, :].rearrange("p n (h q) -> p q h n", q=4)[:, qi]
                P.tt("dve", tY[:, :, :], src, selb, ALU.mult, reads=[part, psB], writes=[tY])
                P.op("dve", lambda e: e.reduce_sum(dst[:, :, qi], tY[:, :, :], axis=AX.X), reads=[tY], writes=[dst])
        own3 = lambda part: part[:, NB, :].rearrange("p (h q) -> p h q", q=4)
        P.tt("dve", ybs[:, :, :], ybs[:, :, :], own3(Ypart), ALU.add, reads=[ybs, Ypart], writes=[ybs])
        P.tt("dve", dbs[:, :, :], dbs[:, :, :], own3(Dpart), ALU.add, reads=[dbs, Dpart], writes=[dbs])
        P.op("dve", lambda e: e.reciprocal(dbs[:, :, :], dbs[:, :, :]), reads=[dbs], writes=[dbs])
        P.tt("dve", ybs[:, :, :], ybs[:, :, :], dbs[:, :, :], ALU.mult, reads=[ybs, dbs], writes=[ybs])
        for h in range(8):
            P.copy("dve", g.yb[h][:, 0:4], ybs[:, h, :], reads=[ybs], writes=[g.yb[h]])
        P.dma("sp", "st_ks", [(d["ks_o"].t[l], g.knf[:, :, :])], reads=[g.knf], writes=[d["ks_o"]])
        P.dma("sp", "st_vs", [(d["vs_o"].t[l], g.vn[0:4, :])], reads=[g.vn], writes=[d["vs_o"]])

    MB = SMB = None
    if cfg.en_moba:
        MB = moba_alloc()
        if cfg.sample:
            SMB = sample_alloc()
        with P.scope() as salloc:
            moba_setup(salloc, MB)
            if cfg.sample:
                sample_setup(salloc, SMB)
    gp = mkgroup("p", NT, 64)
    gp.rstd = P.sbuf("p_rstd", [128, NT], F32)
    allg = [gp]
    gs = None
    if cfg.sample:
        gs = mkgroup("s", 4, 4)
        gs.rstd = P.sbuf("s_rstd", [128, 4], F32)
        allg.append(gs)
    for g in allg:
        g.Zm_e = [[Buf(g.Zm[l].t, f"{g.name}Zm{l}e{e}") for e in (0, 1)] for l in range(L)]
    for g in allg:
        for l in range(L):
            for ee in (0, 1):
                P.op("dve", lambda e: e.memset(g.Zs_e[l][ee][:, :, :], 0.0), writes=[g.Zs_e[l][ee]])
    for l in range(L):
        P.op("dve", lambda e: e.memset(gp.shift[l][:, :], 0.0), writes=[gp.shift[l]])
        P.op("dve", lambda e: e.memset(gp.Zm[l][:, :, :], 0.0), writes=gp.Zm_e[l])
        P.op("dve", lambda e: e.memset(gp.convp[l][:, :, :], 0.0), writes=[gp.convp[l]])
    if gs is not None:
        prs, wr = [], []
        for l in range(L):
            prs += [(gs.shift[l][:, :], d["s_shift"][l]), (gs.Zm[l][:, :, :], d["s_wkv"][l]), (gs.convp[l][:, :, :], d["s_conv"][l])]
            wr += [gs.shift[l], gs.convp[l]] + gs.Zm_e[l]
        for k in range(16):
            prs.append((gs.x[k][:, :], d["xTs"][k * 128:(k + 1) * 128, :]))
            wr.append(gs.x[k])
        P.dma("sp", "ld_s", prs, reads=[d["xTs"]], writes=wr)
        for l in range(L):
            for ee in (0, 1):
                rr = slice(ee * 64, (ee + 1) * 64)
                P.copy("act", gs.Zs_e[l][ee][rr, :, :], gs.Zm[l][rr, :, :], reads=gs.Zm_e[l], writes=[gs.Zs_e[l][ee]])
    for ti in range(ntiles):
        t0 = ti * NT
        P.dma("sp", "ld_x", [(gp.x[k][:, :], xTp[k * 128:(k + 1) * 128, t0:t0 + NT]) for k in range(16)], reads=[xTp], writes=gp.x)
        groups = [gp] + ([gs] if (ti == 0 and gs is not None) else [])
        wstate["tile"] = ti
        for l in range(L):
            layer(l, groups, ti, ti == ntiles - 1)
        P.dma("sp", "st_y", [(yTp[k * 128:(k + 1) * 128, t0:t0 + NT], gp.x[k][:, :]) for k in range(16)], reads=gp.x, writes=[yTp])
        if ti == 0 and gs is not None:
            P.dma("sp", "st_ys", [(d["yTs"][k * 128:(k + 1) * 128, :], gs.x[k][:, :]) for k in range(16)], reads=gs.x, writes=[d["yTs"]])
    for l in range(L):
        P.dma("sp", "st_fin", [(wkv_o[l], gp.Zm[l][:, :, :])], reads=gp.Zm_e[l], writes=[wkv_o])
        P.dma("sp", "st_fin", [(shift_o[l], gp.shift[l][:, :])], reads=[gp.shift[l]], writes=[shift_o])
        P.dma("sp", "st_fin", [(conv_o[l], gp.convp[l][:, :, :])], reads=[gp.convp[l]], writes=[conv_o])
        if gs is not None:
            P.dma("sp", "st_fin", [(d["wkv_os"][l], gs.Zm[l][:, :, :])], reads=gs.Zm_e[l], writes=[d["wkv_os"]])
            P.dma("sp", "st_fin", [(d["shift_os"][l], gs.shift[l][:, :])], reads=[gs.shift[l]], writes=[d["shift_os"]])
            P.dma("sp", "st_fin", [(d["conv_os"][l], gs.convp[l][:, :, :])], reads=[gs.convp[l]], writes=[d["conv_os"]])


def chunk_w(W):
    K, N = W.shape
    return np.ascontiguousarray(W.reshape(K // 128, 128, N // 128, 128).transpose(2, 1, 0, 3))


def colvec(v):
    return np.ascontiguousarray(np.asarray(v).reshape(-1, 128).T)


def prep_shared(inp, L):
    sh = {}
    vecs = np.zeros((L, 128, NV), np.float32)
    lora = np.zeros((L, 128, 2048), np.float32)
    for l in range(L):
        w_in = np.asarray(inp["w_in"][l])
        sh[f"win{l}"] = chunk_w(w_in)
        vc = w_in[:, C_SHIFT + 2048:C_SHIFT + 3072]
        sh[f"wvr{l}"] = np.ascontiguousarray(vc.reshape(16, 128, 4, 256).transpose(2, 1, 0, 3))
        sh[f"wa{l}"] = chunk_w(np.asarray(inp["w_branch_a"][l]))
        sh[f"wb{l}"] = chunk_w(np.asarray(inp["w_branch_b"][l]))
        sh[f"wo{l}"] = chunk_w(np.asarray(inp["w_out"][l]))
        sh[f"wup{l}"] = chunk_w(np.asarray(inp["w_ffn_up"][l]))
        sh[f"wdn{l}"] = chunk_w(np.asarray(inp["w_ffn_down"][l]))
        V = vecs[l]
        V[:, V_LN1:V_LN1 + 16] = colvec(inp["ln_attn_pre"][l])
        V[:, V_LN2:V_LN2 + 16] = colvec(inp["ln_attn_post"][l])
        V[:, V_LN3:V_LN3 + 16] = colvec(inp["ln_ffn_pre"][l])
        V[:, V_LN4:V_LN4 + 16] = colvec(inp["ln_ffn_post"][l])
        V[:, V_MU:V_MU + 26] = colvec(inp["mu_shift"][l])
        V[:, V_DB:V_DB + 8] = colvec(inp["decay_base"][l])
        V[:, V_AB:V_AB + 8] = colvec(inp["a_base"][l])
        V[:, V_KK:V_KK + 8] = colvec(inp["k_k"][l])
        V[:, V_KA:V_KA + 8] = colvec(inp["k_a"][l])
        V[:, V_RK:V_RK + 8] = colvec(np.asarray(inp["r_k"][l]).reshape(-1))
        V[:, V_GW:V_GW + 8] = colvec(inp["gn_w"][l])
        V[:, V_GB:V_GB + 8] = colvec(inp["gn_b"][l])
        for j in range(3):
            V[:, V_CW + j * NFF:V_CW + (j + 1) * NFF] = colvec(inp["ffn_conv_w"][l][j])
        V[:, V_CB:V_CB + NFF] = colvec(inp["ffn_conv_b"][l])
        lora[l, 0:64, 0:1024] = inp["w_decay_up"][l]
        lora[l, 64:128, 0:1024] = inp["w_a_up"][l]
        lora[l, :, 1024:2048] = inp["w_g_up"][l]
    sh["vecs"] = vecs
    sh["lora"] = lora
    sh["cst"] = make_cst()
    return sh


def prep_core(inp, cfg, bp, bs):
    T, L = cfg.T, cfg.L
    m = {}
    m["xTp"] = np.ascontiguousarray(np.asarray(inp["x_prompt"][bp, :T]).T)
    rb = np.asarray(inp["rel_bias"])
    m["relb"] = np.ascontiguousarray(np.concatenate([rb.T, np.full((1, 8), NEG, np.float32), np.zeros((31, 8), np.float32)], axis=0))
    m["onehot"] = make_onehot()
    if cfg.sample:
        m["xTs"] = np.ascontiguousarray(np.asarray(inp["x_sample"][bs]).T)
        ss = np.asarray(inp["state_shift"])[:, bs]
        m["s_shift"] = np.ascontiguousarray(ss.reshape(L, 26, 128).transpose(0, 2, 1))
        sw = np.asarray(inp["state_wkv"])[:, bs]
        m["s_wkv"] = np.ascontiguousarray(sw.reshape(L, 8, 2, 64, 64).transpose(0, 2, 4, 1, 3).reshape(L, 128, 8, 64))
        pt = np.asarray(inp["page_table"])[bs].astype(np.int32)
        m["ptr"] = np.ascontiguousarray(np.tile(pt[None, :], (128, 1)))
        m["ohs"] = make_ohs()
        sc = np.asarray(inp["state_conv"])[:, bs]
        m["s_conv"] = np.ascontiguousarray(sc.reshape(L, 2, NFF, 128).transpose(0, 3, 2, 1))
    return m


def t5_bucket_np(n):
    import math
    n = int(n)
    if n < 16:
        return n
    v = np.float32(np.log(np.float32(n) / np.float32(16))) / np.float32(math.log(128 / 16)) * np.float32(16)
    return min(16 + int(np.int32(v)), 31)


def make_ohs():
    oh = np.zeros((64, 4, 132), np.float32)
    for qi in range(4):
        for k in range(132):
            dist = 128 + qi - k
            if dist >= 0:
                oh[t5_bucket_np(dist), qi, k] = 1.0
            else:
                oh[32, qi, k] = 1.0
    return oh


def make_onehot():
    oh = np.zeros((64, 512), np.float32)
    for m in range(383):
        dist = 255 - m
        if dist >= 0:
            oh[t5_bucket_np(dist), m] = 1.0
        else:
            oh[32, m] = 1.0
    oh[31, 384:512] = 1.0
    return oh


def assemble_core(o, cfg):
    T, L = cfg.T, cfg.L
    res = []
    res.append(("y_prompt", o["yTp"].T[None]))
    res.append(("y_sample", o["yTs"].T[None] if cfg.sample else None))
    res.append(("k_prompt", o["kTo"].transpose(0, 2, 1).reshape(L, 1, T, 8, 128) if cfg.en_moba else None))
    res.append(("v_prompt", o["vo"].reshape(L, 1, T, 8, 128) if cfg.en_moba else None))
    unz = lambda z: z.reshape(L, 2, 64, 8, 64).transpose(0, 3, 1, 4, 2).reshape(L, 1, 16, 64, 64)
    res.append(("wkv_prompt", unz(o["wkv_o"]) if cfg.en_rwkv else None))
    res.append(("shift_prompt", o["shift_o"].transpose(0, 2, 1).reshape(L, 1, 3328) if cfg.en_rwkv else None))
    res.append(("conv_prompt", o["conv_o"].transpose(0, 3, 2, 1).reshape(L, 1, 2, 5632)))
    if cfg.sample:
        res.append(("k_sample", o["ks_o"].transpose(0, 3, 2, 1).reshape(L, 1, 4, 8, 128) if cfg.en_moba else None))
        res.append(("v_sample", o["vs_o"].reshape(L, 1, 4, 8, 128) if cfg.en_moba else None))
        res.append(("wkv_sample", unz(o["wkv_os"]) if cfg.en_rwkv else None))
        res.append(("shift_sample", o["shift_os"].transpose(0, 2, 1).reshape(L, 1, 3328) if cfg.en_rwkv else None))
        res.append(("conv_sample", o["conv_os"].transpose(0, 3, 2, 1).reshape(L, 1, 2, 5632)))
    return res


def kernel(**inp):
    L = 2
    n_pool = int(np.asarray(inp["cache_k"]).shape[1])
    n_pages = int(np.asarray(inp["page_table"]).shape[1])
    cfg = Cfg(T=2048, NT=256, L=L, sample=True, n_pool=n_pool, n_pages=n_pages)
    nc, P = build(cfg)
    sh = prep_shared(inp, L)
    ck = np.ascontiguousarray(np.asarray(inp["cache_k"], dtype=np.float32)).reshape(L, n_pool * 128, 1024)
    cv = np.ascontiguousarray(np.asarray(inp["cache_v"], dtype=np.float32)).reshape(L, n_pool * 128, 1024)
    in_maps = []
    for c in range(8):
        m = dict(sh)
        m.update(prep_core(inp, cfg, c % 4, c))
        for l_ in range(L):
            m[f"cache_k{l_}"] = ck[l_]
            m[f"cache_v{l_}"] = cv[l_]
        in_maps.append(m)
    res = run_bass_kernel_spmd(nc, in_maps, core_ids=list(range(8)))
    per = [assemble_core(r, cfg) for r in res.results]
    outs = []
    for i in range(12):
        cores = range(4) if i in (0, 2, 3, 4, 5, 6) else range(8)
        axis = 0 if i in (0, 1) else 1
        outs.append(np.ascontiguousarray(np.concatenate([per[c][i][1] for c in cores], axis=axis).astype(np.float32)))
    return tuple(outs)
```

```python
import contextlib
import numpy as np
import concourse.bass as bass
import concourse.mybir as mybir
from concourse.bass_utils import run_bass_kernel_spmd

F32 = mybir.dt.float32
BF16 = mybir.dt.bfloat16
I32 = mybir.dt.int32
AF = mybir.ActivationFunctionType
ALU = mybir.AluOpType
AX = mybir.AxisListType


class Buf:
    __slots__ = ("t", "name", "w", "r", "live", "multi", "h")

    def __init__(self, t, name, init_r=None):
        self.t = t
        self.name = name
        self.w = {}
        self.r = dict(init_r or {})
        self.live = False
        self.multi = False
        self.h = None

    def __getitem__(self, idx):
        return self.t[idx]


class Prog:
    def __init__(self, nc):
        self.nc = nc
        self.es = contextlib.ExitStack()
        self.eng = {"pe": nc.tensor, "act": nc.scalar, "dve": nc.vector, "pool": nc.gpsimd, "sp": nc.sync}
        self.sems = {}
        self.cnt = {}
        self.waited = {e: {} for e in self.eng}
        self.freed = {}
        self.nops = {e: 0 for e in self.eng}
        for e in ("pe", "act", "dve", "pool"):
            self._sem(e)
        self.psum_banks = []
        self.psum_i = 0

    def _sem(self, key):
        if key not in self.sems:
            self.sems[key] = self.es.enter_context(self.nc.semaphore("s_" + key))
            self.cnt[key] = 0
        return self.sems[key]

    def sbuf(self, name, shape, dtype, stack=None):
        self.uid = getattr(self, "uid", 0) + 1
        nb = int(np.prod(shape[1:])) * (2 if dtype == BF16 else 4)
        self.acct = getattr(self, "acct", {})
        key = "persist" if stack is None else "scope"
        self.acct[key] = self.acct.get(key, 0) + nb
        self.big = getattr(self, "big", [])
        self.big.append((nb, name, key))
        t = (stack or self.es).enter_context(self.nc.sbuf_tensor(f"sb{self.uid}_{name}", list(shape), dtype))
        return Buf(t, name, self.freed)

    def dram(self, name, shape, dtype, kind):
        t = self.nc.dram_tensor(name, list(shape), dtype, kind=kind)
        b = Buf(t.ap(), name)
        b.h = t
        return b

    def init_psum(self):
        for i in range(8):
            t = self.es.enter_context(self.nc.psum_tensor(f"ps{i}", [128, 512], F32))
            self.psum_banks.append(Buf(t, f"ps{i}"))

    def psum(self):
        while True:
            b = self.psum_banks[self.psum_i % 8]
            self.psum_i += 1
            if not b.live:
                return b

    def psum_hold(self):
        b = self.psum()
        b.live = True
        return b

    @contextlib.contextmanager
    def scope(self):
        st = contextlib.ExitStack()
        bufs = []

        def alloc(name, shape, dtype):
            b = self.sbuf(name, shape, dtype, stack=st)
            bufs.append(b)
            return b
        try:
            yield alloc
        finally:
            fr = dict(self.freed)
            for b in bufs:
                for d in (b.w, b.r):
                    for k, v in d.items():
                        if fr.get(k, 0) < v:
                            fr[k] = v
            self.freed = fr
            st.close()

    def _deps(self, eng, reads, writes):
        deps = {}

        def add(d):
            for k, v in d.items():
                if deps.get(k, 0) < v:
                    deps[k] = v
        for b in reads:
            add(b.w)
        for b in writes:
            if b.multi:
                continue
            add(b.w)
            add(b.r)
        out = []
        wd = self.waited[eng]
        for k, v in deps.items():
            if k == "pe" and eng == "pe":
                continue
            if wd.get(k, 0) >= v:
                continue
            wd[k] = v
            out.append((k, v))
        return out

    def _commit(self, tok, reads, writes):
        k, v = tok
        for b in writes:
            if b.multi:
                if b.w.get(k, 0) < v:
                    b.w[k] = v
                continue
            b.w = {k: v}
            b.r = {}
        for b in reads:
            if b.r.get(k, 0) < v:
                b.r[k] = v

    def op(self, eng, fn, reads=(), writes=()):
        E = self.eng[eng]
        for k, v in self._deps(eng, reads, writes):
            E.wait_ge(self.sems[k], v)
        inst = fn(E)
        self.cnt[eng] += 1
        inst.then_inc(self.sems[eng], 1)
        self.nops[eng] += 1
        self._commit((eng, self.cnt[eng]), reads, writes)

    def dma(self, queue, stream, pairs, reads=(), writes=(), **kw):
        E = self.eng[queue]
        self._sem(stream)
        for k, v in self._deps(queue, reads, writes):
            E.wait_ge(self.sems[k], v)
        for (o, i) in pairs:
            E.dma_start(out=o, in_=i, **kw).then_inc(self.sems[stream], 16)
            self.cnt[stream] += 16
            self.nops[queue] += 1
        self._commit((stream, self.cnt[stream]), reads, writes)

    def dma_gather(self, stream, out_ap, in_ap, idx_ap, reads=(), writes=()):
        E = self.eng["pool"]
        self._sem(stream)
        for k, v in self._deps("pool", reads, writes):
            E.wait_ge(self.sems[k], v)
        E.indirect_dma_start(out=out_ap, out_offset=None, in_=in_ap,
                             in_offset=bass.IndirectOffsetOnAxis(ap=idx_ap, axis=0)).then_inc(self.sems[stream], 16)
        self.cnt[stream] += 16
        self.nops["pool"] += 1
        self._commit((stream, self.cnt[stream]), reads, writes)

    def finish(self):
        E = self.eng["sp"]
        for k, v in self.cnt.items():
            if v > 0 and self.waited["sp"].get(k, 0) < v:
                E.wait_ge(self.sems[k], v)

    def mm(self, ps, out, lhsT, rhs, start, stop, reads):
        self.op("pe", lambda e: e.matmul(out, lhsT, rhs, start=start, stop=stop), reads=reads, writes=[ps])

    def act(self, out, in_, func, reads, writes, eng="act", **kw):
        self.op(eng, lambda e: e.activation(out, in_, func, **kw), reads=reads, writes=writes)

    def ts(self, eng, out, in0, s1, s2, op0, op1, reads, writes):
        if op1 is None:
            self.op(eng, lambda e: e.tensor_scalar(out, in0, s1, None, op0=op0), reads=reads, writes=writes)
        else:
            self.op(eng, lambda e: e.tensor_scalar(out, in0, s1, s2, op0=op0, op1=op1), reads=reads, writes=writes)

    def tt(self, eng, out, in0, in1, op, reads, writes):
        self.op(eng, lambda e: e.tensor_tensor(out, in0, in1, op=op), reads=reads, writes=writes)

    def stt(self, eng, out, in0, scalar, in1, op0, op1, reads, writes):
        self.op(eng, lambda e: e.scalar_tensor_tensor(out, in0, scalar, in1, op0=op0, op1=op1), reads=reads, writes=writes)

    def copy(self, eng, out, in_, reads, writes):
        if eng == "act":
            self.op(eng, lambda e: e.activation(out, in_, AF.Copy), reads=reads, writes=writes)
        else:
            self.op(eng, lambda e: e.tensor_copy(out, in_), reads=reads, writes=writes)
D = 2048
C_A = 1024
C_SHIFT = 3328
D_FF = 5632
NFF = 44
SCALE = 128 ** -0.5
NEG = -30000.0
DEC_K = float(np.exp(-0.5))
GELU_K = float(2.0 * np.sqrt(2.0 / np.pi))

V_LN1, V_LN2, V_LN3, V_LN4 = 0, 16, 32, 48
V_MU = 64
V_DB = 90
V_AB = 98
V_KK = 106
V_KA = 114
V_RK = 122
V_GW = 130
V_GB = 138
V_CW = 146
V_CB = 278
NV = 322


class Cfg:
    def __init__(self, T=2048, NT=512, L=2, sample=True, n_pool=1280, n_pages=128, en_rwkv=True, en_moba=True):
        self.T, self.NT, self.L, self.sample = T, NT, L, sample
        self.n_pool, self.n_pages = n_pool, n_pages
        self.en_rwkv, self.en_moba = en_rwkv, en_moba
EPS = 1e-6
GN_EPS = 64e-5
NW16 = 5


class Group:
    pass


def build(cfg):
    T, NT, L = cfg.T, cfg.NT, cfg.L
    nc = bass.Bass("TRN2", target_bir_lowering=False)
    P = Prog(nc)
    global LASTP
    LASTP = P
    P.init_psum()
    dI = lambda n, s, dt=F32: P.dram(n, s, dt, "ExternalInput")
    dO = lambda n, s, dt=F32: P.dram(n, s, dt, "ExternalOutput")
    xTp = dI("xTp", [D, T])
    win = [dI(f"win{l}", [82, 128, 16, 128]) for l in range(L)]
    wvr = [dI(f"wvr{l}", [4, 128, 16, 256]) for l in range(L)]
    wa = [dI(f"wa{l}", [16, 128, 8, 128]) for l in range(L)]
    wb = [dI(f"wb{l}", [16, 128, 8, 128]) for l in range(L)]
    wo = [dI(f"wo{l}", [16, 128, 16, 128]) for l in range(L)]
    wup = [dI(f"wup{l}", [88, 128, 16, 128]) for l in range(L)]
    wdn = [dI(f"wdn{l}", [16, 128, 44, 128]) for l in range(L)]
    vecs_d = dI("vecs", [L, 128, NV])
    lora_d = dI("lora", [L, 128, 2048])
    cst_d = dI("cst", [128, NCST])
    relb_d = dI("relb", [64, 8])
    onehot_d = dI("onehot", [64, 512])
    yTp = dO("yTp", [D, T])
    kTo = dO("kTo", [L, 1024, T])
    vo = dO("vo", [L, T, 1024])
    wkv_o = dO("wkv_o", [L, 128, 8, 64])
    shift_o = dO("shift_o", [L, 128, 26])
    conv_o = dO("conv_o", [L, 128, NFF, 2])
    tz_d = [P.dram(f"tz_scr{h}", [128, 384], F32, "Internal") for h in range(8)]
    for b_ in (kTo, vo, wkv_o, shift_o, conv_o, yTp):
        b_.multi = True
    extra = {}
    if getattr(cfg, "debug", False):
        extra["dbg_o"] = dO("dbg_o", [128, 8, NT])
        extra["dbg_ya"] = dO("dbg_ya", [8, 128, NT], BF16)
        extra["dbg_kr"] = dO("dbg_kr", [8, 128, NT * 2], BF16)
        extra["dbg_x"] = dO("dbg_x", [3, 128, NT])
    if cfg.sample:
        extra["xTs"] = dI("xTs", [D, 4])
        extra["ptr"] = dI("ptr", [128, cfg.n_pages], I32)
        extra["ohs"] = dI("ohs", [64, 4, 132])
        for l_ in range(L):
            extra[f"cache_k{l_}"] = dI(f"cache_k{l_}", [cfg.n_pool * 128, 1024])
            extra[f"cache_v{l_}"] = dI(f"cache_v{l_}", [cfg.n_pool * 128, 1024])
        extra["ks_o"] = dO("ks_o", [L, 128, 8, 4])
        extra["vs_o"] = dO("vs_o", [L, 4, 1024])
        extra["ks_o"].multi = True
        extra["vs_o"].multi = True
        extra["s_shift"] = dI("s_shift", [L, 128, 26])
        extra["s_wkv"] = dI("s_wkv", [L, 128, 8, 64])
        extra["s_conv"] = dI("s_conv", [L, 128, NFF, 2])
        extra["yTs"] = dO("yTs", [D, 4])
        extra["wkv_os"] = dO("wkv_os", [L, 128, 8, 64])
        extra["shift_os"] = dO("shift_os", [L, 128, 26])
        extra["conv_os"] = dO("conv_os", [L, 128, NFF, 2])
        for k_ in ("wkv_os", "shift_os", "conv_os"):
            extra[k_].multi = True

    with nc.Block() as block:
        @block.sync
        def _(_e):
            emit(cfg, nc, P, locals_ := dict(xTp=xTp, win=win, wvr=wvr, wa=wa, wb=wb, wo=wo, wup=wup, wdn=wdn, vecs_d=vecs_d,
                                          lora_d=lora_d, cst_d=cst_d, relb_d=relb_d, onehot_d=onehot_d, yTp=yTp, kTo=kTo, vo=vo,
                                          wkv_o=wkv_o, shift_o=shift_o, conv_o=conv_o, tz=tz_d, **extra))
            P.finish()
    P.es.close()
    return nc, P


CST_ID = 0
CST_MAB = 128
CST_ML = 640
CST_I8 = 1152
CST_RM = 1664
CST_OHQ = 2304
CST_PIDX = 2816
CST_MAB4 = 2176
CST_ML4 = 2240
CST_I84 = 2272
NCST = 2820


def make_cst():
    c = np.zeros((128, NCST), np.float32)
    c[:, CST_ID:CST_ID + 128] = np.eye(128, dtype=np.float32)
    s = np.arange(64)[:, None]
    t = np.arange(64)[None, :]
    mab = np.concatenate([(s < t), (s <= t)], axis=1).astype(np.float32)
    c[:64, CST_MAB:CST_MAB + 512] = np.tile(mab, (1, 4))
    ml = (t < s).astype(np.float32)
    c[:64, CST_ML:CST_ML + 512] = np.tile(ml, (1, 8))
    c[:64, CST_I8:CST_I8 + 512] = np.tile(np.eye(64, dtype=np.float32), (1, 8))
    rm = np.ones(512, np.float32)
    rm[::64] = 0.0
    c[:, CST_RM:CST_RM + 512] = rm[None, :]
    s4 = np.arange(4)[:, None]
    t4 = np.arange(4)[None, :]
    c[:4, CST_MAB4:CST_MAB4 + 64] = np.tile(np.concatenate([(s4 < t4), (s4 <= t4)], axis=1).astype(np.float32), (1, 8))
    c[:4, CST_ML4:CST_ML4 + 32] = np.tile((t4 < s4).astype(np.float32), (1, 8))
    c[:4, CST_I84:CST_I84 + 32] = np.tile(np.eye(4, dtype=np.float32), (1, 8))
    for qi in range(4):
        c[qi, CST_OHQ + qi * 128:CST_OHQ + (qi + 1) * 128] = 1.0
    c[:, CST_PIDX] = np.arange(128, dtype=np.float32)
    return c


def emit(cfg, nc, P, d):
    T, NT, L = cfg.T, cfg.NT, cfg.L
    NS = NT // 128
    ntiles = T // NT
    xTp, win, wvr, wa, wb, wo, wup, wdn = d["xTp"], d["win"], d["wvr"], d["wa"], d["wb"], d["wo"], d["wup"], d["wdn"]
    yTp, kTo, vo, wkv_o, shift_o, conv_o = d["yTp"], d["kTo"], d["vo"], d["wkv_o"], d["shift_o"], d["conv_o"]

    cst = P.sbuf("cst", [128, NCST], F32)
    vec = [P.sbuf(f"vec{l}", [128, NV], F32) for l in range(L)]
    lora1 = P.sbuf("lora", [128, 2048], F32)
    lora = [lora1 for l in range(L)]
    P.dma("sp", "ld_c", [(cst[:, :], d["cst_d"][:, :])] + [(vec[l][:, :], d["vecs_d"][l]) for l in range(L)]
, reads=[d["cst_d"]], writes=[cst] + vec)
    cstb = P.sbuf("cstb", [128, 128], BF16)
    P.copy("dve", cstb[:, :], cst[:, CST_ID:CST_ID + 128], reads=[cst], writes=[cstb])
    ones_f = P.sbuf("ones_f", [128, 128], F32)
    P.op("dve", lambda e: e.memset(ones_f[:, :], 1.0), writes=[ones_f])
    bones = P.sbuf("bones", [128, 128], F32)
    P.op("dve", lambda e: e.memset(bones[:, :], 0.0), writes=[bones])
    P.op("dve", lambda e: e.memset(bones[0:64, 0:64], 1.0), writes=[bones])
    P.op("dve", lambda e: e.memset(bones[64:128, 64:128], 1.0), writes=[bones])
    identf = lambda: cst[:, CST_ID:CST_ID + 128]
    identb = lambda: cstb[:, :]

    w16 = [P.sbuf(f"w16_{i}", [128, 16, 128], BF16) for i in range(NW16)]
    w44 = []
    wvb = []
    wstate = {"i16": 0, "i44": 0, "iv": 0, "tile": 0}

    wcache = {}
    wq = {"i": 0}

    def getw(src, idx, kc):
        if src.name not in wcache:
            shp = [int(x) for x in src.t.shape]
            c = P.dram("wc_" + src.name, shp, BF16, "Internal")
            c.multi = True
            wcache[src.name] = c
        cache = wcache[src.name]
        first = wstate["tile"] == 0
        if kc == 44:
            k = wstate["i44"] % 2
            b = w44[k]
            wstate["i44"] += 1
            sname, dst = f"w44s{k}", b[:, :, :]
        elif kc == "v":
            k = 0
            b = wvb[0]
            sname, dst = "wvs", b[:, :, :]
        else:
            k = wstate["i16"] % NW16
            b = w16[k]
            wstate["i16"] += 1
            sname, dst = f"w16s{k}", b[:, 0:kc, :]
        if first:
            P.dma("pool", sname, [(dst, src[idx])], reads=[src], writes=[b])
            P.dma("sp", sname + "c", [(cache.t[idx], dst)], reads=[b], writes=[cache])
        else:
            wq["i"] += 1
            if wq["i"] % 2:
                P.dma("pool", sname, [(dst, cache.t[idx])], reads=[cache], writes=[b])
            else:
                P.dma("sp", sname + "h", [(dst, cache.t[idx])], reads=[cache], writes=[b])
        return b

    def projm(src, idx, kc, groups, ins):
        w = getw(src, idx, kc)
        res = []
        for g, hin in zip(groups, ins):
            ps = P.psum()
            for k in range(kc):
                P.mm(ps, ps[:, 0:g.N], w[:, k, :], hin[k][:, 0:g.N], k == 0, k == kc - 1, reads=[w, hin[k]])
            res.append(ps)
        return res

    def mkgroup(name, N, C):
        g = Group()
        g.name, g.N, g.C = name, N, C
        g.NCH = N // C
        g.nlev = int(np.log2(C))
        g.x = [P.sbuf(f"{name}_x{k}", [128, N], F32) for k in range(16)]
        g.h = [P.sbuf(f"{name}_h{k}", [128, N], BF16) for k in range(16)]
        g.tmpi = 0
        g.tmps = [P.sbuf(f"{name}_t{k}", [128, N], F32) for k in range(6)]
        g.shift = [P.sbuf(f"{name}_sh{l}", [128, 26], F32) for l in range(L)]
        g.Zm = [P.sbuf(f"{name}_Zm{l}", [128, 8, 64], F32) for l in range(L)]
        g.Zs_e = [[P.sbuf(f"{name}_Zs{l}e{e}", [128, 8, 64], BF16) for e in (0, 1)] for l in range(L)]
        g.convp = [P.sbuf(f"{name}_cv{l}", [128, NFF, 2], F32) for l in range(L)]
        g.ya = [P.sbuf(f"{name}_ya{k}", [128, N], BF16) for k in range(8)]
        g.yb = [P.sbuf(f"{name}_yb{k}", [128, N], BF16) for k in range(8)]
        return g

    def tmp(g):
        b = g.tmps[g.tmpi % len(g.tmps)]
        g.tmpi += 1
        return b

    def sumsq_rstd(g, X, rstd):
        N = g.N
        ps = P.psum()
        for k in range(16):
            sq = tmp(g)
            P.act(sq[:, 0:N], X[k][:, 0:N], AF.Square, reads=[X[k]], writes=[sq])
            P.mm(ps, ps[:, 0:N], ones_f[:, :], sq[:, 0:N], k == 0, k == 15, reads=[ones_f, sq])
        P.ts("dve", rstd[:, 0:N], ps[:, 0:N], 1.0 / D, EPS, ALU.mult, ALU.add, reads=[ps], writes=[rstd])
        P.op("dve", lambda e: e.reciprocal(rstd[:, 0:N], rstd[:, 0:N]), reads=[rstd], writes=[rstd])
        P.act(rstd[:, 0:N], rstd[:, 0:N], AF.Sqrt, reads=[rstd], writes=[rstd])

    def norm_pre(g, l, vcol):
        N = g.N
        rstd = g.rstd
        sumsq_rstd(g, g.x, rstd)
        for k in range(16):
            P.stt("dve", g.h[k][:, 0:N], g.x[k][:, 0:N], vec[l][:, vcol + k:vcol + k + 1], rstd[:, 0:N], ALU.mult, ALU.mult,
                  reads=[g.x[k], vec[l], rstd], writes=[g.h[k]])

    def norm_post_add(g, l, vcol, M):
        N = g.N
        rstd = g.rstd
        sumsq_rstd(g, M, rstd)
        for k in range(16):
            t = tmp(g)
            P.stt("dve", t[:, 0:N], M[k][:, 0:N], vec[l][:, vcol + k:vcol + k + 1], rstd[:, 0:N], ALU.mult, ALU.mult,
                  reads=[M[k], vec[l], rstd], writes=[t])
            P.tt("dve", g.x[k][:, 0:N], g.x[k][:, 0:N], t[:, 0:N], ALU.add, reads=[g.x[k], t], writes=[g.x[k]])

    def shiftmix(g, l, ch, ps, out, uext):
        N = g.N
        P.copy("act", uext[:, 1:N + 1], ps[:, 0:N], reads=[ps], writes=[uext])
        P.copy("dve", uext[:, 0:1], g.shift[l][:, ch:ch + 1], reads=[g.shift[l], uext], writes=[uext])
        P.copy("dve", g.shift[l][:, ch:ch + 1], uext[:, N:N + 1], reads=[uext, g.shift[l]], writes=[g.shift[l]])
        dd = tmp(g)
        P.tt("dve", dd[:, 0:N], uext[:, 0:N], uext[:, 1:N + 1], ALU.subtract, reads=[uext], writes=[dd])
        P.stt("dve", out[:, 0:N], dd[:, 0:N], vec[l][:, V_MU + ch:V_MU + ch + 1], uext[:, 1:N + 1], ALU.mult, ALU.add,
              reads=[dd, vec[l], uext], writes=[out])

    def layer(l, groups, tile_i, last):
        t0 = tile_i * NT
        for g in groups:
            norm_pre(g, l, V_LN1)
        hin = [g.h for g in groups]
        with P.scope() as alloc:
            for g in groups:
                N = g.N
                g.KR = [alloc(f"{g.name}_KR{i}", [128, g.NCH, 2 * g.C], BF16) for i in range(8)]
                g.KT = [alloc(f"{g.name}_KT{i}", [128, N], BF16) for i in range(8)]
                g.BT = [alloc(f"{g.name}_BT{i}", [128, N], BF16) for i in range(8)]
                g.VF = [alloc(f"{g.name}_VF{i}", [128, N], BF16) for i in range(8)]
                g.G = [alloc(f"{g.name}_G{i}", [128, N], F32) for i in range(8)]
                g.bonus = [alloc(f"{g.name}_bo{i}", [128, N], F32) for i in range(8)]
                g.gC = alloc(f"{g.name}_gC", [128, 8, g.NCH], F32)
                g.oT = alloc(f"{g.name}_oT", [128, 8, N], F32)
                g.wk = [alloc(f"{g.name}_wk{i}", [128, N], F32) for i in range(2)]
            if cfg.en_rwkv:
                P.dma("sp", "ld_lora", [(lora1[:, :], d["lora_d"][l])], reads=[d["lora_d"]], writes=[lora1])
                with P.scope() as alloc2:
                    for g in groups:
                        N = g.N
                        g.uext = [alloc2(f"{g.name}_ue{i}", [128, N + 1], F32) for i in range(2)]
                        g.LA = alloc2(f"{g.name}_LA", [128, N], F32)
                        g.LG = alloc2(f"{g.name}_LG", [128, N], F32)
                        g.xr = alloc2(f"{g.name}_xr", [128, N], F32)
                        g.xk = alloc2(f"{g.name}_xk", [128, N], F32)
                        g.xv = alloc2(f"{g.name}_xv", [128, N], F32)
                        g.wk = [alloc2(f"{g.name}_wk{i}", [128, N], F32) for i in range(10)]
                    rwkv_pre(l, groups, hin)
                for g in groups:
                    with P.scope() as alloc2:
                        C = g.C
                        S = Group()
                        nm = f"{g.name}_cb"
                        S.Vtok = alloc2(nm + "V", [64, 1024], BF16)
                        S.Ktok = alloc2(nm + "K", [64, 1024], BF16)
                        S.Btok = alloc2(nm + "B", [64, 1024], BF16)
                        S.AkBk = [alloc2(nm + f"AkBk{e}", [64, 8, 2 * C], BF16) for e in (0, 1)]
                        S.AbBb = [alloc2(nm + f"AbBb{e}", [64, 8, 2 * C], BF16) for e in (0, 1)]
                        S.Pm = [[alloc2(nm + f"P{e}{k}", [64, 8, C], BF16) for k in (0, 1)] for e in (0, 1)]
                        S.Qm = [[alloc2(nm + f"Q{e}{k}", [64, 8, C], BF16) for k in (0, 1)] for e in (0, 1)]
                        S.Nm = [alloc2(nm + f"Nm{e}", [64, 8, C], F32) for e in (0, 1)]
                        S.Ns = [alloc2(nm + f"Ns{e}", [64, 8, C], BF16) for e in (0, 1)]
                        S.RHS = [alloc2(nm + f"RHS{e}", [64, 8, 64], BF16) for e in (0, 1)]
                        S.Un = [alloc2(nm + f"Un{e}", [64, 8, 64], BF16) for e in (0, 1)]
                        g.cb = [S, S]
                        rwkv_scan(l, g)
                    rwkv_out(l, g)
            else:
                for g in groups:
                    for k in range(8):
                        P.op("dve", lambda e: e.memset(g.ya[k][:, :], 0.0), writes=[g.ya[k]])
        with P.scope() as alloc:
            if cfg.en_moba:
                moba(l, groups, hin, tile_i, alloc)
            else:
                for g in groups:
                    for k in range(8):
                        P.op("dve", lambda e: e.memset(g.yb[k][:, :], 0.0), writes=[g.yb[k]])
        with P.scope() as alloc:
            for g in groups:
                g.mix = [alloc(f"{g.name}_mix{k}", [128, g.N], BF16) for k in range(16)]
                g.m2 = [alloc(f"{g.name}_m2{k}", [128, g.N], F32) for k in range(16)]
            for n in range(16):
                psA = projm(wa[l], n, 8, groups, [g.ya for g in groups])
                psB = projm(wb[l], n, 8, groups, [g.yb for g in groups])
                psGa = projm(win[l], 50 + n, 16, groups, hin)
                psGb = projm(win[l], 66 + n, 16, groups, hin)
                for gi, g in enumerate(groups):
                    N = g.N
                    ga, gb = tmp(g), tmp(g)
                    P.act(ga[:, 0:N], psGa[gi][:, 0:N], AF.Sigmoid, reads=[psGa[gi]], writes=[ga])
                    P.act(gb[:, 0:N], psGb[gi][:, 0:N], AF.Sigmoid, reads=[psGb[gi]], writes=[gb])
                    P.tt("dve", ga[:, 0:N], ga[:, 0:N], psA[gi][:, 0:N], ALU.mult, reads=[ga, psA[gi]], writes=[ga])
                    P.tt("dve", gb[:, 0:N], gb[:, 0:N], psB[gi][:, 0:N], ALU.mult, reads=[gb, psB[gi]], writes=[gb])
                    P.tt("dve", g.mix[n][:, 0:N], ga[:, 0:N], gb[:, 0:N], ALU.add, reads=[ga, gb], writes=[g.mix[n]])
            for n in range(16):
                pss = projm(wo[l], n, 16, groups, [g.mix for g in groups])
                for gi, g in enumerate(groups):
                    P.copy("act", g.m2[n][:, 0:g.N], pss[gi][:, 0:g.N], reads=[pss[gi]], writes=[g.m2[n]])
            for g in groups:
                norm_post_add(g, l, V_LN2, g.m2)
        for g in groups:
            norm_pre(g, l, V_LN3)
        with P.scope() as alloc:
            for g in groups:
                g.act = [alloc(f"{g.name}_act{k}", [128, g.N], BF16) for k in range(NFF)]
                g.f = [alloc(f"{g.name}_f{k}", [128, g.N], F32) for k in range(16)]
                g.ext = [alloc(f"{g.name}_ext{k}", [128, g.N + 2], F32) for k in range(2)]
            w44[:] = [alloc(f"w44_{i}", [128, 44, 128], BF16) for i in range(2)]
            for j in range(NFF):
                psU = projm(wup[l], j, 16, groups, hin)
                psG = projm(wup[l], NFF + j, 16, groups, hin)
                for gi, g in enumerate(groups):
                    N = g.N
                    ext = g.ext[j % 2]
                    P.copy("act", ext[:, 2:N + 2], psU[gi][:, 0:N], reads=[psU[gi]], writes=[ext])
                    P.copy("dve", ext[:, 0:2], g.convp[l][:, j, :], reads=[g.convp[l], ext], writes=[ext])
                    P.copy("dve", g.convp[l][:, j, :], ext[:, N:N + 2], reads=[ext, g.convp[l]], writes=[g.convp[l]])
                    c = tmp(g)
                    cw = lambda jj: vec[l][:, V_CW + jj * NFF + j:V_CW + jj * NFF + j + 1]
                    P.ts("dve", c[:, 0:N], ext[:, 0:N], cw(0), vec[l][:, V_CB + j:V_CB + j + 1], ALU.mult, ALU.add,
                         reads=[ext, vec[l]], writes=[c])
                    P.stt("dve", c[:, 0:N], ext[:, 1:N + 1], cw(1), c[:, 0:N], ALU.mult, ALU.add, reads=[ext, vec[l], c], writes=[c])
                    P.stt("dve", c[:, 0:N], ext[:, 2:N + 2], cw(2), c[:, 0:N], ALU.mult, ALU.add, reads=[ext, vec[l], c], writes=[c])
                    p2 = tmp(g)
                    P.act(p2[:, 0:N], c[:, 0:N], AF.Square, reads=[c], writes=[p2])
                    P.ts("dve", p2[:, 0:N], p2[:, 0:N], 0.044715, 1.0, ALU.mult, ALU.add, reads=[p2], writes=[p2])
                    P.tt("dve", p2[:, 0:N], p2[:, 0:N], c[:, 0:N], ALU.mult, reads=[p2, c], writes=[p2])
                    P.act(p2[:, 0:N], p2[:, 0:N], AF.Sigmoid, reads=[p2], writes=[p2], scale=GELU_K)
                    P.tt("dve", p2[:, 0:N], p2[:, 0:N], c[:, 0:N], ALU.mult, reads=[p2, c], writes=[p2])
                    P.tt("dve", g.act[j][:, 0:N], p2[:, 0:N], psG[gi][:, 0:N], ALU.mult, reads=[p2, psG[gi]], writes=[g.act[j]])
            for n in range(16):
                pss = projm(wdn[l], n, 44, groups, [g.act for g in groups])
                for gi, g in enumerate(groups):
                    P.copy("act", g.f[n][:, 0:g.N], pss[gi][:, 0:g.N], reads=[pss[gi]], writes=[g.f[n]])
            for g in groups:
                norm_post_add(g, l, V_LN4, g.f)

    def rwkv_pre(l, groups, hin):
        for ch in (24, 25):
            pss = projm(win[l], ch, 16, groups, hin)
            for gi, g in enumerate(groups):
                N = g.N
                xs = tmp(g)
                shiftmix(g, l, ch, pss[gi], xs, g.uext[ch % 2])
                if ch == 24:
                    P.act(g.LA[0:64, 0:N], xs[0:64, 0:N], AF.Tanh, reads=[xs], writes=[g.LA])
                    P.copy("dve", g.LA[64:128, 0:N], xs[64:128, 0:N], reads=[xs, g.LA], writes=[g.LA])
                else:
                    P.act(g.LG[:, 0:N], xs[:, 0:N], AF.Sigmoid, reads=[xs], writes=[g.LG])
        for hp in range(8):
            for which, ch in (("xr", hp), ("xk", 8 + hp), ("xv", 16 + hp)):
                pss = projm(win[l], ch, 16, groups, hin)
                for gi, g in enumerate(groups):
                    shiftmix(g, l, ch, pss[gi], getattr(g, which), g.uext[ch % 2])
            for g in groups:
                rwkv_prep(l, g, hp)

    def rwkv_prep(l, g, hp):
        N, C, NCH = g.N, g.C, g.NCH
        cs = slice(hp * 128, (hp + 1) * 128)
        cg = slice(1024 + hp * 128, 1024 + (hp + 1) * 128)
        v = vec[l]
        col = lambda base: v[:, base + hp:base + hp + 1]
        wk = g.wk
        ps_d, ps_a, ps_g = P.psum(), P.psum(), P.psum()
        P.mm(ps_d, ps_d[:, 0:N], lora[l][0:64, cs], g.LA[0:64, 0:N], True, True, reads=[lora[l], g.LA])
        P.mm(ps_a, ps_a[:, 0:N], lora[l][64:128, cs], g.LA[64:128, 0:N], True, True, reads=[lora[l], g.LA])
        P.mm(ps_g, ps_g[:, 0:N], lora[l][:, cg], g.LG[:, 0:N], True, True, reads=[lora[l], g.LG])
        sg, cum, epos, eneg, eprev, a = wk[0], wk[1], wk[2], wk[3], wk[4], wk[5]
        P.act(sg[:, 0:N], ps_d[:, 0:N], AF.Sigmoid, reads=[ps_d, v], writes=[sg], bias=col(V_DB))
        P.act(a[:, 0:N], ps_a[:, 0:N], AF.Sigmoid, reads=[ps_a, v], writes=[a], bias=col(V_AB))
        P.copy("act", g.G[hp][:, 0:N], ps_g[:, 0:N], reads=[ps_g], writes=[g.G[hp]])
        rm = cst[:, CST_RM:CST_RM + N]
        P.op("dve", lambda e: e.tensor_tensor_scan(cum[:, 0:N], rm, sg[:, 0:N], 0.0, ALU.mult, ALU.add),
             reads=[cst, sg], writes=[cum])
        P.act(epos[:, 0:N], cum[:, 0:N], AF.Exp, reads=[cum], writes=[epos], scale=-DEC_K)
        P.act(eneg[:, 0:N], cum[:, 0:N], AF.Exp, reads=[cum], writes=[eneg], scale=DEC_K)
        P.tt("dve", eprev[:, 0:N], cum[:, 0:N], sg[:, 0:N], ALU.subtract, reads=[cum, sg], writes=[eprev])
        P.act(eprev[:, 0:N], eprev[:, 0:N], AF.Exp, reads=[eprev], writes=[eprev], scale=-DEC_K)
        P.copy("dve", g.gC[:, hp, :], epos[:, 0:N].rearrange("p (c t) -> p c t", t=C)[:, :, C - 1], reads=[epos], writes=[g.gC])
        kk0, sq, kap = wk[6], wk[7], wk[8]
        P.ts("dve", kk0[:, 0:N], g.xk[:, 0:N], col(V_KK), None, ALU.mult, None, reads=[g.xk, v], writes=[kk0])
        P.act(sq[:, 0:N], kk0[:, 0:N], AF.Square, reads=[kk0], writes=[sq])
        ps_s = P.psum()
        P.mm(ps_s, ps_s[:, 0:N], bones[:, :], sq[:, 0:N], True, True, reads=[bones, sq])
        P.ts("dve", sq[:, 0:N], ps_s[:, 0:N], 1e-24, None, ALU.max, None, reads=[ps_s], writes=[sq])
        P.op("dve", lambda e: e.reciprocal(sq[:, 0:N], sq[:, 0:N]), reads=[sq], writes=[sq])
        P.act(sq[:, 0:N], sq[:, 0:N], AF.Sqrt, reads=[sq], writes=[sq])
        P.tt("dve", kap[:, 0:N], kk0[:, 0:N], sq[:, 0:N], ALU.mult, reads=[kk0, sq], writes=[kap])
        t1, kpr = wk[9], wk[6]
        P.ts("dve", t1[:, 0:N], a[:, 0:N], -1.0, col(V_KA), ALU.add, ALU.mult, reads=[a, v], writes=[t1])
        P.stt("dve", kpr[:, 0:N], t1[:, 0:N], 1.0, g.xk[:, 0:N], ALU.add, ALU.mult, reads=[t1, g.xk], writes=[kpr])
        bb = wk[7]
        P.tt("dve", bb[:, 0:N], kap[:, 0:N], a[:, 0:N], ALU.mult, reads=[kap, a], writes=[bb])
        rk = wk[9]
        P.stt("dve", rk[:, 0:N], g.xr[:, 0:N], col(V_RK), kpr[:, 0:N], ALU.mult, ALU.mult, reads=[g.xr, v, kpr], writes=[rk])
        ps_b = P.psum()
        P.mm(ps_b, ps_b[:, 0:N], bones[:, :], rk[:, 0:N], True, True, reads=[bones, rk])
        P.tt("dve", g.bonus[hp][:, 0:N], ps_b[:, 0:N], g.xv[:, 0:N], ALU.mult, reads=[ps_b, g.xv], writes=[g.bonus[hp]])
        r3 = lambda b: b[:, 0:N].rearrange("p (c t) -> p c t", t=C)
        P.tt("dve", g.KR[hp][:, :, 0:C], r3(kap), r3(eprev), ALU.mult, reads=[kap, eprev], writes=[g.KR[hp]])
        P.tt("dve", g.KR[hp][:, :, C:2 * C], r3(g.xr), r3(epos), ALU.mult, reads=[g.xr, epos, g.KR[hp]], writes=[g.KR[hp]])
        P.tt("dve", g.KT[hp][:, 0:N], kpr[:, 0:N], eneg[:, 0:N], ALU.mult, reads=[kpr, eneg], writes=[g.KT[hp]])
        P.tt("dve", g.BT[hp][:, 0:N], bb[:, 0:N], eneg[:, 0:N], ALU.mult, reads=[bb, eneg], writes=[g.BT[hp]])
        P.copy("act", g.VF[hp][:, 0:N], g.xv[:, 0:N], reads=[g.xv], writes=[g.VF[hp]])

    def rwkv_scan(l, g):
        N, C, NCH, nlev = g.N, g.C, g.NCH, g.nlev
        HB2 = min(8, 512 // (2 * C))
        if C == 64:
            mab = cst[0:C, CST_MAB:CST_MAB + 512].rearrange("p (h w) -> p h w", w=2 * C)
            ml = cst[0:C, CST_ML:CST_ML + 512].rearrange("p (h w) -> p h w", w=C)
            i8 = cst[0:C, CST_I8:CST_I8 + 512].rearrange("p (h w) -> p h w", w=C)
        else:
            mab = cst[0:C, CST_MAB4:CST_MAB4 + 64].rearrange("p (h w) -> p h w", w=2 * C)
            ml = cst[0:C, CST_ML4:CST_ML4 + 32].rearrange("p (h w) -> p h w", w=C)
            i8 = cst[0:C, CST_I84:CST_I84 + 32].rearrange("p (h w) -> p h w", w=C)
        r3 = lambda ps, w, n=8: ps[0:C, 0:n * w].rearrange("p (h w) -> p h w", w=w)
        Zs_e = g.Zs_e[l]
        Zm_e = g.Zm_e[l]
        Zmt = g.Zm[l]

        def indep(c):
            S = g.cb[c % 2]
            cs = slice(c * C, (c + 1) * C)
            for (src, dst) in ((g.VF, S.Vtok), (g.KT, S.Ktok), (g.BT, S.Btok)):
                for hs in (0, 4):
                    ps = P.psum()
                    for j in range(4):
                        P.mm(ps, ps[0:C, j * 128:(j + 1) * 128], src[hs + j][:, cs], identb(), True, True, reads=[src[hs + j], cstb])
                    P.copy("act", dst[0:C, hs * 128:(hs + 4) * 128], ps[0:C, 0:512], reads=[ps], writes=[dst])
                yield
            for e in (0, 1):
                rows = slice(e * 64, (e + 1) * 64)
                for (lh, dst) in ((g.KT, S.AkBk[e]), (g.BT, S.AbBb[e])):
                    for hs in range(0, 8, HB2):
                        ps = P.psum()
                        for j in range(HB2):
                            hp = hs + j
                            P.mm(ps, ps[0:C, j * 2 * C:(j + 1) * 2 * C], lh[hp][rows, cs], g.KR[hp][rows, c, :], True, True,
                                 reads=[lh[hp], g.KR[hp]])
                        P.tt("dve", dst[0:C, hs:hs + HB2, :], r3(ps, 2 * C, HB2), mab[:, 0:HB2, :], ALU.mult, reads=[ps, cst], writes=[dst])
                ps = P.psum()
                for hp in range(8):
                    P.mm(ps, ps[0:C, hp * C:(hp + 1) * C], g.KR[hp][rows, c, 0:C], g.BT[hp][rows, cs], True, True,
                         reads=[g.KR[hp], g.BT[hp]])
                P.tt("dve", S.Pm[e][0][0:C, :, :], r3(ps, C), ml, ALU.mult, reads=[ps, cst], writes=[S.Pm[e][0]])
                yield
                P.tt("dve", S.Nm[e][0:C, :, :], i8, S.AbBb[e][0:C, :, 0:C], ALU.subtract, reads=[cst, S.AbBb[e]], writes=[S.Nm[e]])
                P.copy("act", S.Ns[e][0:C, :, :], S.Nm[e][0:C, :, :], reads=[S.Nm[e]], writes=[S.Ns[e]])
                Pk = lambda k, hp: S.Pm[e][k % 2][0:C, hp, :]
                Pb_ = lambda k: S.Pm[e][k % 2]
                Qk = lambda k, hp: (S.AbBb[e][0:C, hp, 0:C] if k == 0 else S.Qm[e][k % 2][0:C, hp, :])
                Qb_ = lambda k: (S.AbBb[e] if k == 0 else S.Qm[e][k % 2])
                for k in range(nlev):
                    if k >= 1:
                        ps = P.psum()
                        for hp in range(8):
                            P.mm(ps, ps[0:C, hp * C:(hp + 1) * C], Pk(k, hp), S.Ns[e][0:C, hp, :], True, True, reads=[Pb_(k), S.Ns[e]])
                        P.tt("dve", S.Nm[e][0:C, :, :], r3(ps, C), S.Nm[e][0:C, :, :], ALU.add, reads=[ps, S.Nm[e]], writes=[S.Nm[e]])
                        P.copy("act", S.Ns[e][0:C, :, :], S.Nm[e][0:C, :, :], reads=[S.Nm[e]], writes=[S.Ns[e]])
                    if k < nlev - 1:
                        ps = P.psum()
                        for hp in range(8):
                            P.mm(ps, ps[0:C, hp * C:(hp + 1) * C], Qk(k, hp), Pk(k, hp), True, True, reads=[Qb_(k), Pb_(k)])
                        P.copy("act", S.Pm[e][(k + 1) % 2][0:C, :, :], r3(ps, C), reads=[ps], writes=[S.Pm[e][(k + 1) % 2]])
                        if k + 1 < nlev - 1:
                            ps = P.psum()
                            for hp in range(8):
                                P.mm(ps, ps[0:C, hp * C:(hp + 1) * C], Pk(k, hp), Qk(k, hp), True, True, reads=[Qb_(k), Pb_(k)])
                            P.copy("dve", S.Qm[e][(k + 1) % 2][0:C, :, :], r3(ps, C), reads=[ps], writes=[S.Qm[e][(k + 1) % 2]])
                    yield

        def dep(c, filler):
            S = g.cb[c % 2]
            cs = slice(c * C, (c + 1) * C)

            def fill(n=3):
                for _ in range(n):
                    try:
                        next(filler)
                    except StopIteration:
                        break
            for e in (0, 1):
                rows = slice(e * 64, (e + 1) * 64)
                vcol = lambda hp: slice(hp * 128 + e * 64, hp * 128 + e * 64 + 64)
                ps = P.psum()
                for hp in range(8):
                    o = ps[0:C, hp * 64:(hp + 1) * 64]
                    P.mm(ps, o, g.KR[hp][:, c, 0:C], Zs_e[e][:, hp, :], True, False, reads=[g.KR[hp], Zs_e[e]])
                    P.mm(ps, o, S.AkBk[e][0:C, hp, 0:C], S.Vtok[0:C, vcol(hp)], False, True, reads=[S.AkBk[e], S.Vtok])
                P.copy("act", S.RHS[e][0:C, :, :], r3(ps, 64), reads=[ps], writes=[S.RHS[e]])
                ps = P.psum()
                for hp in range(8):
                    P.mm(ps, ps[0:C, hp * 64:(hp + 1) * 64], S.Ns[e][0:C, hp, :], S.RHS[e][0:C, hp, :], True, True, reads=[S.Ns[e], S.RHS[e]])
                P.ts("dve", S.Un[e][0:C, :, :], r3(ps, 64), -1.0, None, ALU.mult, None, reads=[ps], writes=[S.Un[e]])
                ps = P.psum()
                for hp in range(8):
                    o = ps[0:64, hp * C:(hp + 1) * C]
                    P.mm(ps, o, Zs_e[e][:, hp, :], g.KR[hp][:, c, C:2 * C], True, False, reads=[Zs_e[e], g.KR[hp]])
                    P.mm(ps, o, S.Vtok[0:C, vcol(hp)], S.AkBk[e][0:C, hp, C:2 * C], False, False, reads=[S.Vtok, S.AkBk[e]])
                    P.mm(ps, o, S.Un[e][0:C, hp, :], S.AbBb[e][0:C, hp, C:2 * C], False, True, reads=[S.Un[e], S.AbBb[e]])
                P.copy("act", g.oT[rows, :, cs], ps[0:64, 0:8 * C].rearrange("p (h w) -> p h w", w=C), reads=[ps], writes=[g.oT])
                ps = P.psum()
                for hp in range(8):
                    o = ps[0:64, hp * 64:(hp + 1) * 64]
                    P.mm(ps, o, S.Ktok[0:C, vcol(hp)], S.Vtok[0:C, vcol(hp)], True, False, reads=[S.Ktok, S.Vtok])
                    P.mm(ps, o, S.Btok[0:C, vcol(hp)], S.Un[e][0:C, hp, :], False, True, reads=[S.Btok, S.Un[e]])
                z3 = ps[0:64, 0:512].rearrange("p (h w) -> p h w", w=64)
                P.tt("dve", Zmt[rows, :, :], z3, Zmt[rows, :, :], ALU.add, reads=[ps, Zm_e[e]], writes=[Zm_e[e]])
                P.tt("dve", Zmt[rows, :, :], Zmt[rows, :, :], g.gC[rows, :, c:c + 1].to_broadcast([64, 8, 64]), ALU.mult,
                     reads=[Zm_e[e], g.gC], writes=[Zm_e[e]])
                P.copy("act", Zs_e[e][rows, :, :], Zmt[rows, :, :], reads=[Zm_e[e]], writes=[Zs_e[e]])
                fill(6)

        for c in range(NCH):
            for _ in indep(c):
                pass
            dep(c, iter(()))

    def rwkv_out(l, g):
        N = g.N
        if "dbg_o" in d and g.N == NT and l == 0:
            P.dma("sp", "dbg0", [(d["dbg_o"][:, :, :], g.oT[:, :, :])], reads=[g.oT], writes=[d["dbg_o"]])
            P.dma("sp", "dbg1", [(d["dbg_kr"][hp], g.KR[hp][:, :, :].rearrange("p c w -> p (c w)")) for hp in range(8)], reads=g.KR, writes=[d["dbg_kr"]])
            P.dma("sp", "dbg2", [(d["dbg_x"][0], g.xr[:, :]), (d["dbg_x"][1], g.xk[:, :]), (d["dbg_x"][2], g.xv[:, :])], reads=[g.xr, g.xk, g.xv], writes=[d["dbg_x"]])
        v = vec[l]
        wk = g.wk
        for hp in range(8):
            col = lambda base: v[:, base + hp:base + hp + 1]
            o = g.oT[:, hp, 0:N]
            ps = P.psum()
            P.mm(ps, ps[:, 0:N], bones[:, :], o, True, True, reads=[bones, g.oT])
            cen, sq = wk[0], wk[1]
            P.stt("dve", cen[:, 0:N], ps[:, 0:N], -1.0 / 64, o, ALU.mult, ALU.add, reads=[ps, g.oT], writes=[cen])
            P.act(sq[:, 0:N], cen[:, 0:N], AF.Square, reads=[cen], writes=[sq])
            ps2 = P.psum()
            P.mm(ps2, ps2[:, 0:N], bones[:, :], sq[:, 0:N], True, True, reads=[bones, sq])
            P.ts("dve", sq[:, 0:N], ps2[:, 0:N], 1.0 / 64, GN_EPS, ALU.mult, ALU.add, reads=[ps2], writes=[sq])
            P.op("dve", lambda e: e.reciprocal(sq[:, 0:N], sq[:, 0:N]), reads=[sq], writes=[sq])
            P.act(sq[:, 0:N], sq[:, 0:N], AF.Sqrt, reads=[sq], writes=[sq])
            P.tt("dve", cen[:, 0:N], cen[:, 0:N], sq[:, 0:N], ALU.mult, reads=[cen, sq], writes=[cen])
            P.ts("dve", cen[:, 0:N], cen[:, 0:N], col(V_GW), col(V_GB), ALU.mult, ALU.add, reads=[cen, v], writes=[cen])
            P.tt("dve", cen[:, 0:N], cen[:, 0:N], g.bonus[hp][:, 0:N], ALU.add, reads=[cen, g.bonus[hp]], writes=[cen])
            P.tt("dve", g.ya[hp][:, 0:N], cen[:, 0:N], g.G[hp][:, 0:N], ALU.mult, reads=[cen, g.G[hp]], writes=[g.ya[hp]])
        if "dbg_o" in d and g.N == NT and l == 0:
            P.dma("sp", "dbg3", [(d["dbg_ya"][hp], g.ya[hp][:, :]) for hp in range(8)], reads=g.ya, writes=[d["dbg_ya"]])

    def moba_alloc():
        M = Group()
        M.F = [P.sbuf(f"Ftab{h}", [128, 256], F32) for h in range(8)]
        M.B31x = P.sbuf("B31x", [128, 8, 8], F32)
        M.kms = [P.sbuf(f"kms{l}", [128, 8, max(T // 256, 1)], BF16) for l in range(L)]
        M.kmf = [P.sbuf(f"kmf{l}", [128, 8, max(T // 256, 1)], F32) for l in range(L)]
        return M

    def sample_alloc():
        SM = Group()
        SM.Idx = P.sbuf("Idx", [128, cfg.n_pages], I32)
        SM.BL = P.sbuf("BiasLast", [128, 8, 4], F32)
        SM.BO = P.sbuf("BiasOwn", [4, 8, 4], F32)
        SM.ones_b = P.sbuf("ones_b", [128, 128], BF16)
        return SM

    def moba_setup(salloc, M):
        relb = salloc("relb", [64, 8], F32)
        oh = salloc("oh", [64, 512], F32)
        P.dma("sp", "ld_c2", [(relb[:, :], d["relb_d"][:, :]), (oh[:, :], d["onehot_d"][:, :])], reads=[d["relb_d"]], writes=[relb, oh])
        M.relb = relb
        if True:
            alloc = salloc
            rep = [alloc(f"rep{i}", [64, 128], F32) for i in range(2)]
            yr = [alloc(f"yr{i}", [128, 384], F32) for i in range(2)]
            for h in range(8):
                r_ = rep[h % 2]
                P.copy("dve", r_[:, :], relb[:, h:h + 1].to_broadcast([64, 128]), reads=[relb], writes=[r_])
                ps = P.psum()
                P.mm(ps, ps[:, 0:384], r_[:, :], oh[:, 0:384], True, True, reads=[r_, oh])
                y_ = yr[h % 2]
                P.copy("act", y_[:, :], ps[:, 0:384], reads=[ps], writes=[y_])
                tz = d["tz"][h]
                P.dma("sp", f"tzw{h}", [(tz[:, :], y_[:, :])], reads=[y_], writes=[tz])
                src = bass.AP(tz.h, 127, [[383, 128], [1, 256]])
                P.dma("sp", f"tzr{h}", [(M.F[h][:, :], src)], reads=[tz], writes=[M.F[h]])
            ps = P.psum()
            P.mm(ps, ps[:, 0:8], oh[:, 384:512], relb[:, 0:8], True, True, reads=[oh, relb])
            b31 = alloc("b31", [128, 8], F32)
            P.copy("act", b31[:, :], ps[:, 0:8], reads=[ps], writes=[b31])
            P.copy("dve", M.B31x[:, :, :], b31[:, :].unsqueeze(2).to_broadcast([128, 8, 8]), reads=[b31], writes=[M.B31x])
        return M

    def moba(l, groups, hin, tile_i, alloc):
        gp = groups[0]
        gsm = groups[1] if len(groups) > 1 else None
        N = NT
        t0 = tile_i * NT
        M = MB
        QT = [alloc(f"QT{h}", [128, N], BF16) for h in range(8)]
        TK = t0 + NT
        KA = [alloc(f"KA{h}", [128, TK], BF16) for h in range(8)]
        VA = alloc("VA", [128, TK // 128, 1024], BF16)
        kf = [alloc(f"kf{i}", [128, N], F32) for i in range(2)]
        vf = [alloc(f"vf{i}", [128, 256], F32) for i in range(2)]
        nbuf = 2 if TK <= 1024 else 1
        Pb = [alloc(f"Pb{i}", [128, TK], BF16) for i in range(nbuf)] * (3 - nbuf)
        PT = [alloc(f"PT{i}", [128, TK // 128, 128], BF16) for i in range(nbuf)] * (3 - nbuf)
        wvb[:] = [alloc("wvb", [128, 16, 256], BF16)]
        Gs = alloc("Gs", [128, 8, 8], F32)
        top8 = alloc("top8", [128, 8, 8], F32)
        SB = alloc("SB", [128, 8, 8], F32)
        SBb = alloc("SBb", [128, 8, 8], F32)
        dcol = [alloc(f"dcol{i}", [128, 20], F32) for i in range(2)]
        dsum = [alloc(f"dsum{i}", [128, 1], F32) for i in range(2)]
        Dg = [alloc(f"Dg{i}", [128, 128], BF16) for i in range(2)]
        tS = [alloc(f"tS{i}", [128, 128], F32) for i in range(2)]
        if gsm is not None:
            gsm.QT = [alloc(f"sQT{h}", [128, 4], BF16) for h in range(8)]
            gsm.knf = alloc("s_knf", [128, 8, 4], F32)
            gsm.knb = alloc("s_knb", [128, 8, 4], BF16)
            gsm.vn = alloc("s_vn", [4, 1024], F32)
            gsm.vnb = alloc("s_vnb", [4, 1024], BF16)
        if getattr(cfg, "moba_stop", 9) < 0.2:
            for k in range(8):
                P.op("dve", lambda e: e.memset(gp.yb[k][:, :], 0.0), writes=[gp.yb[k]])
            return
        if t0 > 0:
            P.dma("pool", "kh", [(KA[h][:, 0:t0], kTo.t[l, h * 128:(h + 1) * 128, 0:t0]) for h in range(8)], reads=[kTo], writes=KA)
            P.dma("pool", "vh", [(VA[:, 0:t0 // 128, :], vo.t[l, 0:t0, :].rearrange("(k p) f -> p k f", p=128))], reads=[vo], writes=[VA])
        for h in range(8):
            pss = projm(win[l], 26 + h, 16, groups, hin)
            P.copy("act", QT[h][:, 0:N], pss[0][:, 0:N], reads=[pss[0]], writes=[QT[h]])
            if gsm is not None:
                P.copy("act", gsm.QT[h][:, 0:4], pss[1][:, 0:4], reads=[pss[1]], writes=[gsm.QT[h]])
        for h in range(8):
            pss = projm(win[l], 34 + h, 16, groups, hin)
            k_ = kf[h % 2]
            P.copy("act", k_[:, 0:N], pss[0][:, 0:N], reads=[pss[0]], writes=[k_])
            P.copy("dve", KA[h][:, t0:t0 + N], k_[:, 0:N], reads=[k_], writes=[KA[h]])
            P.dma("sp", f"st_k{h % 2}", [(kTo.t[l, h * 128:(h + 1) * 128, t0:t0 + N], k_[:, 0:N])], reads=[k_], writes=[kTo])
            for b in range(N // 256):
                bi = t0 // 256 + b
                P.op("dve", lambda e: e.reduce_sum(M.kmf[l][:, h, bi:bi + 1], k_[:, b * 256:(b + 1) * 256], axis=AX.X),
                     reads=[k_], writes=[M.kmf[l]])
                P.copy("dve", M.kms[l][:, h, bi:bi + 1], M.kmf[l][:, h, bi:bi + 1], reads=[M.kmf[l]], writes=[M.kms[l]])
            if gsm is not None:
                P.copy("act", gsm.knf[:, h, :], pss[1][:, 0:4], reads=[pss[1]], writes=[gsm.knf])
                P.copy("dve", gsm.knb[:, h, :], gsm.knf[:, h, :], reads=[gsm.knf], writes=[gsm.knb])
        if getattr(cfg, "moba_stop", 9) < 0.7:
            for k in range(8):
                P.op("dve", lambda e: e.memset(gp.yb[k][:, :], 0.0), writes=[gp.yb[k]])
            return
        vi = 0
        for gc in range(4):
            w = getw(wvr[l], gc, "v")
            gcs = slice(gc * 256, (gc + 1) * 256)
            for sub in range(NS):
                ps = P.psum()
                for k in range(16):
                    P.mm(ps, ps[:, 0:256], gp.h[k][:, sub * 128:(sub + 1) * 128], w[:, k, :], k == 0, k == 15, reads=[gp.h[k], w])
                v_ = vf[vi % 2]
                P.copy("act", v_[:, 0:256], ps[:, 0:256], reads=[ps], writes=[v_])
                P.copy("dve", VA[:, t0 // 128 + sub, gcs], v_[:, 0:256], reads=[v_], writes=[VA])
                P.dma("sp", f"st_v{vi % 2}", [(vo.t[l, t0 + sub * 128:t0 + (sub + 1) * 128, gcs], v_[:, 0:256])], reads=[v_], writes=[vo])
                vi += 1
            if gsm is not None:
                ps = P.psum()
                for k in range(16):
                    P.mm(ps, ps[0:4, 0:256], gsm.h[k][:, 0:4], w[:, k, :], k == 0, k == 15, reads=[gsm.h[k], w])
                P.copy("act", gsm.vn[0:4, gcs], ps[0:4, 0:256], reads=[ps], writes=[gsm.vn])
                P.copy("dve", gsm.vnb[0:4, gcs], gsm.vn[0:4, gcs], reads=[gsm.vn], writes=[gsm.vnb])
        it = 0
        if getattr(cfg, "moba_stop", 9) < 2:
            for k in range(8):
                P.op("dve", lambda e: e.memset(gp.yb[k][:, :], 0.0), writes=[gp.yb[k]])
            return
        for qs in range(NS):
            qsg = t0 // 128 + qs
            own = qsg // 2
            qcols = slice(qs * 128, (qs + 1) * 128)
            if own > 3:
                psG = P.psum()
                for h in range(8):
                    P.mm(psG, psG[:, h * 8:h * 8 + own], QT[h][:, qcols], M.kms[l][:, h, 0:own], True, True, reads=[QT[h], M.kms[l]])
                P.op("dve", lambda e: e.memset(Gs[:, :, :], -1e30), writes=[Gs])
                P.copy("dve", Gs[:, :, 0:own], psG[:, 0:64].rearrange("p (h w) -> p h w", w=8)[:, :, 0:own], reads=[psG, Gs], writes=[Gs])
                for h in range(8):
                    P.op("dve", lambda e: e.max(top8[:, h, :], Gs[:, h, :]), reads=[Gs], writes=[top8])
                for h in range(8):
                    P.ts("dve", SB[:, h, :], Gs[:, h, :], top8[:, h, 2:3], NEG, ALU.is_lt, ALU.mult, reads=[Gs, top8, SB], writes=[SB])
            else:
                P.op("dve", lambda e: e.memset(SB[:, :, :], 0.0), writes=[SB])
            P.tt("dve", SBb[:, :, :], SB[:, :, :], M.B31x[:, :, :], ALU.add, reads=[SB, M.B31x], writes=[SBb])
            nkt = qsg + 1
            kend = nkt * 128
            for h in range(8):
                pb, pt, dc, ds, dg = Pb[it % 2], PT[it % 2], dcol[it % 2], dsum[it % 2], Dg[it % 2]
                it += 1
                banks = []
                for kg in range((kend + 511) // 512):
                    w_ = min(512, kend - kg * 512)
                    ps = P.psum()
                    P.mm(ps, ps[:, 0:w_], QT[h][:, qcols], KA[h][:, kg * 512:kg * 512 + w_], True, True, reads=[QT[h], KA[h]])
                    banks.append(ps)
                pcol = lambda kt: (banks[kt // 4], (kt % 4) * 128)
                nreg = 0
                P.op("dve", lambda e: e.memset(dc[:, :], 0.0), writes=[dc])
                kt = 0
                while kt < qsg - 1:
                    n = kt // 2
                    span = 2 if (kt % 2 == 0 and kt + 1 < qsg - 1) else 1
                    bk, c0 = pcol(kt)
                    P.act(pb[:, kt * 128:(kt + span) * 128], bk[:, c0:c0 + span * 128], AF.Exp, reads=[bk, SBb], writes=[pb, dc],
                          scale=SCALE, bias=SBb[:, h, n:n + 1], accum_out=dc[:, nreg:nreg + 1])
                    nreg += 1
                    kt += span
                for kt in (qsg - 1, qsg):
                    if kt < 0:
                        continue
                    n = kt // 2
                    bk, c0 = pcol(kt)
                    t_ = tS[nreg % 2]
                    fc = 0 if kt == qsg - 1 else 128
                    P.act(t_[:, :], bk[:, c0:c0 + 128], AF.Copy, reads=[bk], writes=[t_], scale=SCALE)
                    P.tt("dve", t_[:, :], t_[:, :], M.F[h][:, fc:fc + 128], ALU.add, reads=[t_, M.F[h]], writes=[t_])
                    if n < own:
                        P.act(pb[:, kt * 128:(kt + 1) * 128], t_[:, :], AF.Exp, reads=[t_, SB], writes=[pb, dc],
                              bias=SB[:, h, n:n + 1], accum_out=dc[:, nreg:nreg + 1])
                    else:
                        P.act(pb[:, kt * 128:(kt + 1) * 128], t_[:, :], AF.Exp, reads=[t_], writes=[pb, dc], accum_out=dc[:, nreg:nreg + 1])
                    nreg += 1
                P.op("dve", lambda e: e.reduce_sum(ds[:, 0:1], dc[:, 0:nreg], axis=AX.X), reads=[dc], writes=[ds])
                P.op("dve", lambda e: e.reciprocal(ds[:, 0:1], ds[:, 0:1]), reads=[ds], writes=[ds])
                P.ts("dve", dg[:, :], cst[:, CST_ID:CST_ID + 128], ds[:, 0:1], None, ALU.mult, None, reads=[cst, ds], writes=[dg])
                for kg in range(0, nkt, 4):
                    kk_ = min(4, nkt - kg)
                    ps = P.psum()
                    for j in range(kk_):
                        P.mm(ps, ps[:, j * 128:(j + 1) * 128], pb[:, (kg + j) * 128:(kg + j + 1) * 128], dg[:, :], True, True, reads=[pb, dg])
                    P.copy("act" if (kg // 4) % 2 == 0 else "dve", pt[:, kg:kg + kk_, :], ps[:, 0:kk_ * 128].rearrange("p (k w) -> p k w", w=128),
                           reads=[ps], writes=[pt])
                psy = P.psum()
                for kt in range(nkt):
                    P.mm(psy, psy[:, 0:128], VA[:, kt, h * 128:(h + 1) * 128], pt[:, kt, :], kt == 0, kt == nkt - 1, reads=[VA, pt])
                P.copy("act", gp.yb[h][:, qcols], psy[:, 0:128], reads=[psy], writes=[gp.yb[h]])
        if gsm is not None:
            moba_sample(l, gsm, alloc)


    def sample_setup(salloc, SM):
        npg = cfg.n_pages
        ptb = salloc("ptb", [128, npg], I32)
        ohs = salloc("ohs", [64, 4, 132], F32)
        P.dma("sp", "ld_pt", [(ptb[:, :], d["ptr"][:, :]), (ohs[:, :, :], d["ohs"][:, :, :])], reads=[d["ptr"]], writes=[ptb, ohs])
        ptf = salloc("ptf", [128, npg], F32)
        P.copy("dve", ptf[:, :], ptb[:, :], reads=[ptb], writes=[ptf])
        P.ts("dve", ptf[:, :], ptf[:, :], 128.0, cst[:, CST_PIDX:CST_PIDX + 1], ALU.mult, ALU.add, reads=[ptf, cst], writes=[ptf])
        P.copy("dve", SM.Idx[:, :], ptf[:, :], reads=[ptf], writes=[SM.Idx])
        psL = P.psum()
        psO = P.psum()
        for qi in range(4):
            P.mm(psL, psL[:, qi * 8:(qi + 1) * 8], ohs[:, qi, 0:128], MB.relb[:, 0:8], True, True, reads=[ohs, MB.relb])
            P.mm(psO, psO[0:4, qi * 8:(qi + 1) * 8], ohs[:, qi, 128:132], MB.relb[:, 0:8], True, True, reads=[ohs, MB.relb])
        for qi in range(4):
            P.copy("act", SM.BL[:, :, qi], psL[:, qi * 8:(qi + 1) * 8], reads=[psL], writes=[SM.BL])
            P.copy("act", SM.BO[0:4, :, qi], psO[0:4, qi * 8:(qi + 1) * 8], reads=[psO], writes=[SM.BO])
        P.copy("dve", SM.ones_b[:, :], ones_f[:, :], reads=[ones_f], writes=[SM.ones_b])
        return SM

    def moba_sample(l, g, alloc):
        npg = cfg.n_pages
        NB = npg // 2
        SM = SMB
        Kp = [alloc(f"sKp{i}", [128, 1024], F32) for i in range(2)]
        Vp = [alloc(f"sVp{i}", [128, 1024], F32) for i in range(2)]
        Kb = [alloc(f"sKb{i}", [128, 1024], BF16) for i in range(2)]
        Vb = [alloc(f"sVb{i}", [128, 1024], BF16) for i in range(2)]
        KTp = [alloc(f"sKT{i}", [128, 8, 128], BF16) for i in range(2)]
        Ypart = alloc("sYp", [128, NB + 1, 32], F32)
        Dpart = alloc("sDp", [128, NB + 1, 32], F32)
        kmt = alloc("skmt", [128, 8, 2], F32)
        kmS = alloc("skmS", [128, 8, NB], F32)
        kmSb = alloc("skmSb", [128, 8, NB], BF16)
        tE = [alloc(f"stE{i}", [128, 8, 4], F32) for i in range(2)]
        Eb = [alloc(f"sEb{i}", [128, 32], BF16) for i in range(2)]
        ck = d[f"cache_k{l}"].t
        cv = d[f"cache_v{l}"].t
        psY = psD = None
        for i in range(npg):
            par = i % 2
            n = i // 2
            kp, vp, kb, vb, kt_, te, eb = Kp[par], Vp[par], Kb[par], Vb[par], KTp[par], tE[par], Eb[par]
            P.dma_gather(f"gk{par}", kp[:, :], ck, SM.Idx[:, i:i + 1], reads=[d[f"cache_k{l}"], SM.Idx], writes=[kp])
            P.dma_gather(f"gv{par}", vp[:, :], cv, SM.Idx[:, i:i + 1], reads=[d[f"cache_v{l}"], SM.Idx], writes=[vp])
            P.copy("dve", kb[:, :], kp[:, :], reads=[kp], writes=[kb])
            P.copy("act", vb[:, :], vp[:, :], reads=[vp], writes=[vb])
            for hg in (0, 4):
                ps = P.psum()
                for j in range(4):
                    P.mm(ps, ps[:, j * 128:(j + 1) * 128], kb[:, (hg + j) * 128:(hg + j + 1) * 128], identb(), True, True, reads=[kb, cstb])
                P.copy("act" if hg == 0 else "dve", kt_[:, hg:hg + 4, :], ps[:, 0:512].rearrange("p (h w) -> p h w", w=128), reads=[ps], writes=[kt_])
            P.op("dve", lambda e: e.reduce_sum(kmt[:, :, par], kt_[:, :, :], axis=AX.X), reads=[kt_], writes=[kmt])
            if par == 1:
                P.tt("dve", kmS[:, :, n], kmt[:, :, 0], kmt[:, :, 1], ALU.add, reads=[kmt], writes=[kmS])
            ps = P.psum()
            for h in range(8):
                P.mm(ps, ps[:, h * 4:(h + 1) * 4], kt_[:, h, :], g.QT[h][:, 0:4], True, True, reads=[kt_, g.QT[h]])
            bias3 = SM.BL[:, :, :] if i == npg - 1 else MB.B31x[:, :, 0:4]
            P.stt("dve", te[:, :, :], ps[:, 0:32].rearrange("p (h q) -> p h q", q=4), SCALE, bias3, ALU.mult, ALU.add,
                  reads=[ps, SM.BL, MB.B31x], writes=[te])
            P.act(eb[:, :], te[:, :, :].rearrange("p h q -> p (h q)"), AF.Exp, reads=[te], writes=[eb])
            psY = P.psum()
            for h in range(8):
                P.mm(psY, psY[:, h * 4:(h + 1) * 4], vb[:, h * 128:(h + 1) * 128], eb[:, h * 4:(h + 1) * 4], True, True, reads=[vb, eb])
            psD = P.psum()
            P.mm(psD, psD[:, 0:32], SM.ones_b[:, :], eb[:, :], True, True, reads=[SM.ones_b, eb])
            if par == 0:
                P.copy("act", Ypart[:, n, :], psY[:, 0:32], reads=[psY], writes=[Ypart])
                P.copy("act", Dpart[:, n, :], psD[:, 0:32], reads=[psD], writes=[Dpart])
            else:
                P.tt("dve", Ypart[:, n, :], psY[:, 0:32], Ypart[:, n, :], ALU.add, reads=[psY, Ypart], writes=[Ypart])
                P.tt("dve", Dpart[:, n, :], psD[:, 0:32], Dpart[:, n, :], ALU.add, reads=[psD, Dpart], writes=[Dpart])
        ps = P.psum()
        for h in range(8):
            P.mm(ps, ps[0:4, h * 4:(h + 1) * 4], g.knb[:, h, :], g.QT[h][:, 0:4], True, True, reads=[g.knb, g.QT[h]])
        te, eb = tE[0], Eb[0]
        P.stt("dve", te[0:4, :, :], ps[0:4, 0:32].rearrange("p (h q) -> p h q", q=4), SCALE, SM.BO[0:4, :, :], ALU.mult, ALU.add,
              reads=[ps, SM.BO], writes=[te])
        P.act(eb[0:4, :], te[0:4, :, :].rearrange("p h q -> p (h q)"), AF.Exp, reads=[te], writes=[eb])
        psY = P.psum()
        for h in range(8):
            P.mm(psY, psY[:, h * 4:(h + 1) * 4], g.vnb[0:4, h * 128:(h + 1) * 128], eb[0:4, h * 4:(h + 1) * 4], True, True, reads=[g.vnb, eb])
        psD = P.psum()
        P.mm(psD, psD[:, 0:32], SM.ones_b[0:4, :], eb[0:4, :], True, True, reads=[SM.ones_b, eb])
        P.copy("act", Ypart[:, NB, :], psY[:, 0:32], reads=[psY], writes=[Ypart])
        P.copy("act", Dpart[:, NB, :], psD[:, 0:32], reads=[psD], writes=[Dpart])
        sel = alloc("ssel", [4, 8, NB], F32)
        Gq = alloc("sGq", [4, 8, NB], F32)
        t8 = alloc("st8", [4, 8, 8], F32)
        P.copy("dve", kmSb[:, :, :], kmS[:, :, :], reads=[kmS], writes=[kmSb])
        if NB > 3:
            ps = P.psum()
            for h in range(8):
                P.mm(ps, ps[0:4, h * NB:(h + 1) * NB], g.QT[h][:, 0:4], kmSb[:, h, :], True, True, reads=[g.QT[h], kmSb])
            P.copy("dve", Gq[0:4, :, :], ps[0:4, 0:8 * NB].rearrange("p (h n) -> p h n", n=NB), reads=[ps], writes=[Gq])
            for h in range(8):
                P.op("dve", lambda e: e.max(t8[0:4, h, :], Gq[0:4, h, :]), reads=[Gq], writes=[t8])
            for h in range(8):
                P.ts("dve", sel[0:4, h, :], Gq[0:4, h, :], t8[0:4, h, 2:3], None, ALU.is_ge, None, reads=[Gq, t8, sel], writes=[sel])
        else:
            P.op("dve", lambda e: e.memset(sel[0:4, :, :], 1.0), writes=[sel])
        ybs = alloc("sybs", [128, 8, 4], F32)
        dbs = alloc("sdbs", [128, 8, 4], F32)
        tY = alloc("stY", [128, 8, NB], F32)
        for qi in range(4):
            psB = P.psum()
            P.mm(psB, psB[:, 0:8 * NB], cst[0:4, CST_OHQ + qi * 128:CST_OHQ + (qi + 1) * 128], sel[0:4, :, :].rearrange("p h n -> p (h n)"),
                 True, True, reads=[cst, sel])
            selb = psB[:, 0:8 * NB].rearrange("p (h n) -> p h n", n=NB)
            for (part, dst) in ((Ypart, ybs), (Dpart, dbs)):
                src = part[:, 0:NB, :].rearrange("p n (h q) -> p q h n", q=4)[:, qi]
                P.tt("dve", tY[:, :, :], src, selb, ALU.mult, reads=[part, psB], writes=[tY])
                P.op("dve", lambda e: e.reduce_sum(dst[:, :, qi], tY[:, :, :], axis=AX.X), reads=[tY], writes=[dst])
        own3 = lambda part: part[:, NB, :].rearrange("p (h q) -> p h q", q=4)
        P.tt("dve", ybs[:, :, :], ybs[:, :, :], own3(Ypart), ALU.add, reads=[ybs, Ypart], writes=[ybs])
        P.tt("dve", dbs[:, :, :], dbs[:, :, :], own3(Dpart), ALU.add, reads=[dbs, Dpart], writes=[dbs])
        P.op("dve", lambda e: e.reciprocal(dbs[:, :, :], dbs[:, :, :]), reads=[dbs], writes=[dbs])
        P.tt("dve", ybs[:, :, :], ybs[:, :, :], dbs[:, :, :], ALU.mult, reads=[ybs, dbs], writes=[ybs])
        for h in range(8):
            P.copy("dve", g.yb[h][:, 0:4], ybs[:, h, :], reads=[ybs], writes=[g.yb[h]])
        P.dma("sp", "st_ks", [(d["ks_o"].t[l], g.knf[:, :, :])], reads=[g.knf], writes=[d["ks_o"]])
        P.dma("sp", "st_vs", [(d["vs_o"].t[l], g.vn[0:4, :])], reads=[g.vn], writes=[d["vs_o"]])

    MB = SMB = None
    if cfg.en_moba:
        MB = moba_alloc()
        if cfg.sample:
            SMB = sample_alloc()
        with P.scope() as salloc:
            moba_setup(salloc, MB)
            if cfg.sample:
                sample_setup(salloc, SMB)
    gp = mkgroup("p", NT, 64)
    gp.rstd = P.sbuf("p_rstd", [128, NT], F32)
    allg = [gp]
    gs = None
    if cfg.sample:
        gs = mkgroup("s", 4, 4)
        gs.rstd = P.sbuf("s_rstd", [128, 4], F32)
        allg.append(gs)
    for g in allg:
        g.Zm_e = [[Buf(g.Zm[l].t, f"{g.name}Zm{l}e{e}") for e in (0, 1)] for l in range(L)]
    for g in allg:
        for l in range(L):
            for ee in (0, 1):
                P.op("dve", lambda e: e.memset(g.Zs_e[l][ee][:, :, :], 0.0), writes=[g.Zs_e[l][ee]])
    for l in range(L):
        P.op("dve", lambda e: e.memset(gp.shift[l][:, :], 0.0), writes=[gp.shift[l]])
        P.op("dve", lambda e: e.memset(gp.Zm[l][:, :, :], 0.0), writes=gp.Zm_e[l])
        P.op("dve", lambda e: e.memset(gp.convp[l][:, :, :], 0.0), writes=[gp.convp[l]])
    if gs is not None:
        prs, wr = [], []
        for l in range(L):
            prs += [(gs.shift[l][:, :], d["s_shift"][l]), (gs.Zm[l][:, :, :], d["s_wkv"][l]), (gs.convp[l][:, :, :], d["s_conv"][l])]
            wr += [gs.shift[l], gs.convp[l]] + gs.Zm_e[l]
        for k in range(16):
            prs.append((gs.x[k][:, :], d["xTs"][k * 128:(k + 1) * 128, :]))
            wr.append(gs.x[k])
        P.dma("sp", "ld_s", prs, reads=[d["xTs"]], writes=wr)
        for l in range(L):
            for ee in (0, 1):
                rr = slice(ee * 64, (ee + 1) * 64)
                P.copy("act", gs.Zs_e[l][ee][rr, :, :], gs.Zm[l][rr, :, :], reads=gs.Zm_e[l], writes=[gs.Zs_e[l][ee]])
    for ti in range(ntiles):
        t0 = ti * NT
        P.dma("sp", "ld_x", [(gp.x[k][:, :], xTp[k * 128:(k + 1) * 128, t0:t0 + NT]) for k in range(16)], reads=[xTp], writes=gp.x)
        groups = [gp] + ([gs] if (ti == 0 and gs is not None) else [])
        wstate["tile"] = ti
        for l in range(L):
            layer(l, groups, ti, ti == ntiles - 1)
        P.dma("sp", "st_y", [(yTp[k * 128:(k + 1) * 128, t0:t0 + NT], gp.x[k][:, :]) for k in range(16)], reads=gp.x, writes=[yTp])
        if ti == 0 and gs is not None:
            P.dma("sp", "st_ys", [(d["yTs"][k * 128:(k + 1) * 128, :], gs.x[k][:, :]) for k in range(16)], reads=gs.x, writes=[d["yTs"]])
    for l in range(L):
        P.dma("sp", "st_fin", [(wkv_o[l], gp.Zm[l][:, :, :])], reads=gp.Zm_e[l], writes=[wkv_o])
        P.dma("sp", "st_fin", [(shift_o[l], gp.shift[l][:, :])], reads=[gp.shift[l]], writes=[shift_o])
        P.dma("sp", "st_fin", [(conv_o[l], gp.convp[l][:, :, :])], reads=[gp.convp[l]], writes=[conv_o])
        if gs is not None:
            P.dma("sp", "st_fin", [(d["wkv_os"][l], gs.Zm[l][:, :, :])], reads=gs.Zm_e[l], writes=[d["wkv_os"]])
            P.dma("sp", "st_fin", [(d["shift_os"][l], gs.shift[l][:, :])], reads=[gs.shift[l]], writes=[d["shift_os"]])
            P.dma("sp", "st_fin", [(d["conv_os"][l], gs.convp[l][:, :, :])], reads=[gs.convp[l]], writes=[d["conv_os"]])


def chunk_w(W):
    K, N = W.shape
    return np.ascontiguousarray(W.reshape(K // 128, 128, N // 128, 128).transpose(2, 1, 0, 3))


def colvec(v):
    return np.ascontiguousarray(np.asarray(v).reshape(-1, 128).T)


def prep_shared(inp, L):
    sh = {}
    vecs = np.zeros((L, 128, NV), np.float32)
    lora = np.zeros((L, 128, 2048), np.float32)
    for l in range(L):
        w_in = np.asarray(inp["w_in"][l])
        sh[f"win{l}"] = chunk_w(w_in)
        vc = w_in[:, C_SHIFT + 2048:C_SHIFT + 3072]
        sh[f"wvr{l}"] = np.ascontiguousarray(vc.reshape(16, 128, 4, 256).transpose(2, 1, 0, 3))
        sh[f"wa{l}"] = chunk_w(np.asarray(inp["w_branch_a"][l]))
        sh[f"wb{l}"] = chunk_w(np.asarray(inp["w_branch_b"][l]))
        sh[f"wo{l}"] = chunk_w(np.asarray(inp["w_out"][l]))
        sh[f"wup{l}"] = chunk_w(np.asarray(inp["w_ffn_up"][l]))
        sh[f"wdn{l}"] = chunk_w(np.asarray(inp["w_ffn_down"][l]))
        V = vecs[l]
        V[:, V_LN1:V_LN1 + 16] = colvec(inp["ln_attn_pre"][l])
        V[:, V_LN2:V_LN2 + 16] = colvec(inp["ln_attn_post"][l])
        V[:, V_LN3:V_LN3 + 16] = colvec(inp["ln_ffn_pre"][l])
        V[:, V_LN4:V_LN4 + 16] = colvec(inp["ln_ffn_post"][l])
        V[:, V_MU:V_MU + 26] = colvec(inp["mu_shift"][l])
        V[:, V_DB:V_DB + 8] = colvec(inp["decay_base"][l])
        V[:, V_AB:V_AB + 8] = colvec(inp["a_base"][l])
        V[:, V_KK:V_KK + 8] = colvec(inp["k_k"][l])
        V[:, V_KA:V_KA + 8] = colvec(inp["k_a"][l])
        V[:, V_RK:V_RK + 8] = colvec(np.asarray(inp["r_k"][l]).reshape(-1))
        V[:, V_GW:V_GW + 8] = colvec(inp["gn_w"][l])
        V[:, V_GB:V_GB + 8] = colvec(inp["gn_b"][l])
        for j in range(3):
            V[:, V_CW + j * NFF:V_CW + (j + 1) * NFF] = colvec(inp["ffn_conv_w"][l][j])
        V[:, V_CB:V_CB + NFF] = colvec(inp["ffn_conv_b"][l])
        lora[l, 0:64, 0:1024] = inp["w_decay_up"][l]
        lora[l, 64:128, 0:1024] = inp["w_a_up"][l]
        lora[l, :, 1024:2048] = inp["w_g_up"][l]
    sh["vecs"] = vecs
    sh["lora"] = lora
    sh["cst"] = make_cst()
    return sh


def prep_core(inp, cfg, bp, bs):
    T, L = cfg.T, cfg.L
    m = {}
    m["xTp"] = np.ascontiguousarray(np.asarray(inp["x_prompt"][bp, :T]).T)
    rb = np.asarray(inp["rel_bias"])
    m["relb"] = np.ascontiguousarray(np.concatenate([rb.T, np.full((1, 8), NEG, np.float32), np.zeros((31, 8), np.float32)], axis=0))
    m["onehot"] = make_onehot()
    if cfg.sample:
        m["xTs"] = np.ascontiguousarray(np.asarray(inp["x_sample"][bs]).T)
        ss = np.asarray(inp["state_shift"])[:, bs]
        m["s_shift"] = np.ascontiguousarray(ss.reshape(L, 26, 128).transpose(0, 2, 1))
        sw = np.asarray(inp["state_wkv"])[:, bs]
        m["s_wkv"] = np.ascontiguousarray(sw.reshape(L, 8, 2, 64, 64).transpose(0, 2, 4, 1, 3).reshape(L, 128, 8, 64))
        pt = np.asarray(inp["page_table"])[bs].astype(np.int32)
        m["ptr"] = np.ascontiguousarray(np.tile(pt[None, :], (128, 1)))
        m["ohs"] = make_ohs()
        sc = np.asarray(inp["state_conv"])[:, bs]
        m["s_conv"] = np.ascontiguousarray(sc.reshape(L, 2, NFF, 128).transpose(0, 3, 2, 1))
    return m


def t5_bucket_np(n):
    import math
    n = int(n)
    if n < 16:
        return n
    v = np.float32(np.log(np.float32(n) / np.float32(16))) / np.float32(math.log(128 / 16)) * np.float32(16)
    return min(16 + int(np.int32(v)), 31)


def make_ohs():
    oh = np.zeros((64, 4, 132), np.float32)
    for qi in range(4):
        for k in range(132):
            dist = 128 + qi - k
            if dist >= 0:
                oh[t5_bucket_np(dist), qi, k] = 1.0
            else:
                oh[32, qi, k] = 1.0
    return oh


def make_onehot():
    oh = np.zeros((64, 512), np.float32)
    for m in range(383):
        dist = 255 - m
        if dist >= 0:
            oh[t5_bucket_np(dist), m] = 1.0
        else:
            oh[32, m] = 1.0
    oh[31, 384:512] = 1.0
    return oh


def assemble_core(o, cfg):
    T, L = cfg.T, cfg.L
    res = []
    res.append(("y_prompt", o["yTp"].T[None]))
    res.append(("y_sample", o["yTs"].T[None] if cfg.sample else None))
    res.append(("k_prompt", o["kTo"].transpose(0, 2, 1).reshape(L, 1, T, 8, 128) if cfg.en_moba else None))
    res.append(("v_prompt", o["vo"].reshape(L, 1, T, 8, 128) if cfg.en_moba else None))
    unz = lambda z: z.reshape(L, 2, 64, 8, 64).transpose(0, 3, 1, 4, 2).reshape(L, 1, 16, 64, 64)
    res.append(("wkv_prompt", unz(o["wkv_o"]) if cfg.en_rwkv else None))
    res.append(("shift_prompt", o["shift_o"].transpose(0, 2, 1).reshape(L, 1, 3328) if cfg.en_rwkv else None))
    res.append(("conv_prompt", o["conv_o"].transpose(0, 3, 2, 1).reshape(L, 1, 2, 5632)))
    if cfg.sample:
        res.append(("k_sample", o["ks_o"].transpose(0, 3, 2, 1).reshape(L, 1, 4, 8, 128) if cfg.en_moba else None))
        res.append(("v_sample", o["vs_o"].reshape(L, 1, 4, 8, 128) if cfg.en_moba else None))
        res.append(("wkv_sample", unz(o["wkv_os"]) if cfg.en_rwkv else None))
        res.append(("shift_sample", o["shift_os"].transpose(0, 2, 1).reshape(L, 1, 3328) if cfg.en_rwkv else None))
        res.append(("conv_sample", o["conv_os"].transpose(0, 3, 2, 1).reshape(L, 1, 2, 5632)))
    return res


def kernel(**inp):
    L = 2
    n_pool = int(np.asarray(inp["cache_k"]).shape[1])
    n_pages = int(np.asarray(inp["page_table"]).shape[1])
    cfg = Cfg(T=2048, NT=256, L=L, sample=True, n_pool=n_pool, n_pages=n_pages)
    nc, P = build(cfg)
    sh = prep_shared(inp, L)
    ck = np.ascontiguousarray(np.asarray(inp["cache_k"], dtype=np.float32)).reshape(L, n_pool * 128, 1024)
    cv = np.ascontiguousarray(np.asarray(inp["cache_v"], dtype=np.float32)).reshape(L, n_pool * 128, 1024)
    in_maps = []
    for c in range(8):
        m = dict(sh)
        m.update(prep_core(inp, cfg, c % 4, c))
        for l_ in range(L):
            m[f"cache_k{l_}"] = ck[l_]
            m[f"cache_v{l_}"] = cv[l_]
        in_maps.append(m)
    res = run_bass_kernel_spmd(nc, in_maps, core_ids=list(range(8)))
    per = [assemble_core(r, cfg) for r in res.results]
    outs = []
    for i in range(12):
        cores = range(4) if i in (0, 2, 3, 4, 5, 6) else range(8)
        axis = 0 if i in (0, 1) else 1
        outs.append(np.ascontiguousarray(np.concatenate([per[c][i][1] for c in cores], axis=axis).astype(np.float32)))
    return tuple(outs)
```
